# Optimizing a Trainium2 kernel written in Bass

```python
import math
import jax
import jax.numpy as jnp
from jax import lax
import numpy as np

D_MODEL = 2048
BATCH = 4
SEQ = 4096
DEPTH = 2
DEC_BATCH = 16
DEC_SEQ = 16
PAST_LEN = 2048

CHUNK = 64
N_BRANCH = 4
BRANCH_W = D_MODEL // 4
CONV_WIDTH = 3
CONV_DIM = BRANCH_W
RET_HEADS = 4
RET_DK = BRANCH_W // RET_HEADS
RET_DV = BRANCH_W // RET_HEADS
DIFF_HEADS = 4
DIFF_D = BRANCH_W // (2 * DIFF_HEADS)
DIFF_DV = 2 * DIFF_D
MEM_TOKENS = 256
MEM_HEADS = 4
MEM_HD = BRANCH_W // MEM_HEADS
D_FF = 11 * D_MODEL // 4
REL_BUCKETS = 32
REL_MAX_DIST = 128
Q_BLOCK = 128
LN_EPS = 1e-5
ROPE_BASE = 10000.0
NEG_INF = -1e30
DEEPNORM_ALPHA = (2 * DEPTH) ** 0.25
DEEPNORM_BETA = (8 * DEPTH) ** -0.25
IN_SPLITS = (CONV_DIM, CONV_DIM, CONV_DIM,
             RET_HEADS * RET_DK, RET_HEADS * RET_DK, RET_HEADS * RET_DV, RET_HEADS * RET_DV,
             DIFF_HEADS * 2 * DIFF_D, DIFF_HEADS * 2 * DIFF_D, DIFF_HEADS * DIFF_DV,
             MEM_HEADS * MEM_HD)
IN_COLS = sum(IN_SPLITS)

kernel_name = 'hybrid_stream_encoder_step'

F32 = jnp.float32


def layer_norm(x, g, b):
    x32 = x.astype(F32)
    mu = jnp.mean(x32, -1, keepdims=True)
    var = jnp.mean(jnp.square(x32 - mu), -1, keepdims=True)
    return ((x32 - mu) * lax.rsqrt(var + LN_EPS) * g.astype(F32) + b.astype(F32)).astype(x.dtype)


def rms_norm(x, g):
    x32 = x.astype(F32)
    return x32 * lax.rsqrt(jnp.mean(jnp.square(x32), -1, keepdims=True) + LN_EPS) * g.astype(F32)


def head_norm(x, g):
    mu = jnp.mean(x, -1, keepdims=True)
    var = jnp.mean(jnp.square(x - mu), -1, keepdims=True)
    return (x - mu) * lax.rsqrt(var + LN_EPS) * g.astype(F32)


def swiglu(x, w_up, w_down):
    a, b = jnp.split(x @ w_up, 2, axis=-1)
    return (jax.nn.silu(a) * b) @ w_down


def split_cols(x, sizes):
    out, start = [], 0
    for s in sizes:
        out.append(x[..., start:start + s])
        start += s
    return out


def rope(x, pos):
    half = x.shape[-1] // 2
    inv = ROPE_BASE ** (-jnp.arange(half, dtype=F32) / half)
    ang = pos.astype(F32)[:, None] * inv[None, :]
    cos = jnp.cos(ang)[None, :, None, :]
    sin = jnp.sin(ang)[None, :, None, :]
    x1 = x[..., :half].astype(F32)
    x2 = x[..., half:].astype(F32)
    return jnp.concatenate([x1 * cos - x2 * sin, x2 * cos + x1 * sin], -1).astype(x.dtype)


def t5_bucket(rel):
    nb = REL_BUCKETS // 2
    max_exact = nb // 2
    n = jnp.abs(rel)
    nf = jnp.maximum(n, 1).astype(F32)
    large = max_exact + (jnp.log(nf / max_exact) / math.log(REL_MAX_DIST / max_exact)
                         * (nb - max_exact)).astype(jnp.int32)
    large = jnp.minimum(large, nb - 1)
    return jnp.where(rel > 0, nb, 0) + jnp.where(n < max_exact, n, large)


def short_conv_branch(b_gate, c_gate, h, conv_w, conv_prev):
    u = c_gate * h
    bsz, length, _ = u.shape
    if conv_prev is None:
        prev = jnp.zeros((bsz, CONV_WIDTH - 1, CONV_DIM), u.dtype)
    else:
        prev = conv_prev.astype(u.dtype)
    up = jnp.concatenate([prev, u], axis=1)
    z = sum(conv_w[j] * up[:, j:j + length] for j in range(CONV_WIDTH))
    return b_gate * z, up[:, length:]


def retention(q, k, v, s0):
    bsz, length, heads, _ = q.shape
    dv = v.shape[-1]
    c = CHUNK if length % CHUNK == 0 else length
    n = length // c
    log_g = jnp.log1p(-jnp.exp2(-5.0 - jnp.arange(heads, dtype=F32)))
    idx = jnp.arange(c, dtype=F32)
    rel = idx[:, None] - idx[None, :]
    intra = jnp.where(rel[None] >= 0, jnp.exp(jnp.maximum(rel, 0.0)[None] * log_g[:, None, None]), 0.0)
    q_dec = jnp.exp((idx[:, None] + 1.0) * log_g[None, :])
    k_dec = jnp.exp((c - 1.0 - idx)[:, None] * log_g[None, :])
    c_dec = jnp.exp(c * log_g)

    def to_chunks(t):
        return jnp.moveaxis(t.astype(F32).reshape(bsz, n, c, heads, t.shape[-1]), 1, 0)

    def step(s, xs):
        qc, kc, vc = xs
        att = jnp.einsum('bihd,bjhd->bhij', qc, kc) * intra
        o = (jnp.einsum('bhij,bjhe->bihe', att, vc)
             + jnp.einsum('bihd,bhde->bihe', qc, s) * q_dec[None, :, :, None])
        s = s * c_dec[:, None, None] + jnp.einsum('bjhd,bjhe->bhde', kc * k_dec[None, :, :, None], vc)
        return s, o

    s_fin, o = lax.scan(step, s0.astype(F32), (to_chunks(q), to_chunks(k), to_chunks(v)))
    return jnp.moveaxis(o, 0, 1).reshape(bsz, length, heads, dv), s_fin


def diff_attend(q, k, v, q_pos, k_pos, rel_bias, lam):
    s = jnp.einsum('bqhcd,bkhcd->bchqk', q, k).astype(F32) * (DIFF_D ** -0.5)
    bias = jnp.transpose(rel_bias[t5_bucket(k_pos[None, :] - q_pos[:, None])], (2, 0, 1)).astype(F32)
    allowed = (k_pos[None, :] // CHUNK) <= (q_pos[:, None] // CHUNK)
    s = jnp.where(allowed, s + bias, NEG_INF)
    p = jax.nn.softmax(s, axis=-1)
    a = p[:, 0] - lam * p[:, 1]
    return jnp.einsum('bhqk,bkhe->bqhe', a.astype(v.dtype), v)


def diff_attention(q, k, v, q_pos, k_pos, rel_bias, lam):
    bsz, sq = q.shape[:2]
    if sq <= Q_BLOCK or sq % Q_BLOCK:
        return diff_attend(q, k, v, q_pos, k_pos, rel_bias, lam)
    nb = sq // Q_BLOCK
    qb = jnp.moveaxis(q.reshape((bsz, nb, Q_BLOCK) + q.shape[2:]), 1, 0)
    pb = q_pos.reshape(nb, Q_BLOCK)
    ob = lax.map(lambda a: diff_attend(a[0], k, v, a[1], k_pos, rel_bias, lam), (qb, pb))
    return jnp.moveaxis(ob, 0, 1).reshape((bsz, sq) + ob.shape[3:])


def mem_attention(q, mk, mv):
    s = jnp.einsum('bqhd,bkhd->bhqk', q, mk).astype(F32) * (MEM_HD ** -0.5)
    p = jax.nn.softmax(s, axis=-1)
    return jnp.einsum('bhqk,bkhd->bqhd', p.astype(mv.dtype), mv)


def memory_kv(mem, w_mem_kv):
    k, v = jnp.split(mem @ w_mem_kv, 2, axis=-1)
    shape = mem.shape[:2] + (MEM_HEADS, MEM_HD)
    return k.reshape(shape), v.reshape(shape)


def token_mixer(h, pos, layer_idx, lw, rel_bias, conv_prev, ret_prev, k_past, v_past, past_pos, mem_k, mem_v):
    bsz, length, _ = h.shape
    (cb, cc, ch, rq, rk, rv, rg, dq, dk, dv, mq) = split_cols(h @ lw['w_in'], IN_SPLITS)
    y_a, conv_new = short_conv_branch(cb, cc, ch, lw['conv_w'], conv_prev)
    rq = rope(rq.reshape(bsz, length, RET_HEADS, RET_DK), pos)
    rk = rope(rk.reshape(bsz, length, RET_HEADS, RET_DK), pos) * (RET_DK ** -0.5)
    rv = rv.reshape(bsz, length, RET_HEADS, RET_DV)
    s0 = jnp.zeros((bsz, RET_HEADS, RET_DK, RET_DV), F32) if ret_prev is None else ret_prev
    ro, ret_new = retention(rq, rk, rv, s0)
    ro = head_norm(ro, lw['ret_gn_g'].reshape(RET_HEADS, RET_DV))
    y_b = jax.nn.silu(rg) * ro.reshape(bsz, length, RET_HEADS * RET_DV).astype(h.dtype)
    dk_rows = dk.reshape(bsz, length, DIFF_HEADS, 2 * DIFF_D)
    dv_rows = dv.reshape(bsz, length, DIFF_HEADS, DIFF_DV)
    if k_past is None:
        k_all, v_all, k_pos = dk_rows, dv_rows, pos
    else:
        k_all = jnp.concatenate([k_past.astype(dk_rows.dtype), dk_rows], axis=1)
        v_all = jnp.concatenate([v_past.astype(dv_rows.dtype), dv_rows], axis=1)
        k_pos = jnp.concatenate([past_pos, pos])
    lam_init = 0.8 - 0.6 * math.exp(-0.3 * layer_idx)
    lp = lw['diff_lambda'].astype(F32)
    lam = jnp.exp(jnp.sum(lp[0] * lp[1])) - jnp.exp(jnp.sum(lp[2] * lp[3])) + lam_init
    do = diff_attention(dq.reshape(bsz, length, DIFF_HEADS, 2, DIFF_D),
                        k_all.reshape(bsz, -1, DIFF_HEADS, 2, DIFF_D), v_all,
                        pos, k_pos, rel_bias, lam)
    do = rms_norm(do, lw['diff_subln_g']) * (1.0 - lam_init)
    y_c = do.reshape(bsz, length, DIFF_HEADS * DIFF_DV).astype(h.dtype)
    y_d = mem_attention(mq.reshape(bsz, length, MEM_HEADS, MEM_HD), mem_k, mem_v).reshape(bsz, length, BRANCH_W)
    merged = 0
    for i, y in enumerate((y_a, y_b, y_c, y_d)):
        merged = merged + jax.nn.sigmoid(h @ lw['w_gate'][i] + lw['b_gate'][i]) * (y @ lw['w_branch'][i])
    return merged @ lw['w_o'], (conv_new, ret_new, dk_rows, dv_rows)


def encoder_layer(x, pos, layer_idx, lw, rel_bias, conv_prev, ret_prev, k_past, v_past, past_pos, mem_k, mem_v):
    x = layer_norm(DEEPNORM_ALPHA * x + 0.5 * swiglu(x, lw['ffn1_w_up'], lw['ffn1_w_down']), lw['ln1_g'], lw['ln1_b'])
    mix, new_state = token_mixer(x, pos, layer_idx, lw, rel_bias, conv_prev, ret_prev, k_past, v_past, past_pos, mem_k, mem_v)
    x = layer_norm(DEEPNORM_ALPHA * x + mix, lw['ln2_g'], lw['ln2_b'])
    x = layer_norm(DEEPNORM_ALPHA * x + 0.5 * swiglu(x, lw['ffn2_w_up'], lw['ffn2_w_down']), lw['ln3_g'], lw['ln3_b'])
    return x, new_state


def setup_inputs(seed: int = 0) -> dict:
    key = jax.random.key(seed)
    ks = jax.random.split(key, 32)
    beta = DEEPNORM_BETA

    def nrm(k, shape, scale):
        return jax.random.normal(k, shape, F32) * scale

    return {
        'x_prompt': nrm(ks[0], (BATCH, SEQ, D_MODEL), 1.0),
        'x_sample': nrm(ks[1], (DEC_BATCH, DEC_SEQ, D_MODEL), 1.0),
        'state_conv': nrm(ks[2], (DEPTH, DEC_BATCH, CONV_WIDTH - 1, CONV_DIM), 1.0),
        'state_ret': nrm(ks[3], (DEPTH, DEC_BATCH, RET_HEADS, RET_DK, RET_DV), 1.0),
        'cache_diff_k': nrm(ks[4], (DEPTH, DEC_BATCH, PAST_LEN, DIFF_HEADS, 2 * DIFF_D), 1.0),
        'cache_diff_v': nrm(ks[5], (DEPTH, DEC_BATCH, PAST_LEN, DIFF_HEADS, DIFF_DV), 1.0),
        'cache_mem_k': nrm(ks[6], (DEPTH, DEC_BATCH, MEM_TOKENS, MEM_HEADS, MEM_HD), 1.0),
        'cache_mem_v': nrm(ks[7], (DEPTH, DEC_BATCH, MEM_TOKENS, MEM_HEADS, MEM_HD), 1.0),
        'mem_prompt': nrm(ks[8], (BATCH, MEM_TOKENS, D_MODEL), 1.0),
        'ffn1_w_up': nrm(ks[9], (DEPTH, D_MODEL, 2 * D_FF), beta * D_MODEL ** -0.5),
        'ffn1_w_down': nrm(ks[10], (DEPTH, D_FF, D_MODEL), beta * D_FF ** -0.5),
        'ln1_g': 1.0 + nrm(ks[11], (DEPTH, D_MODEL), 0.02),
        'ln1_b': nrm(ks[12], (DEPTH, D_MODEL), 0.02),
        'w_in': nrm(ks[13], (DEPTH, D_MODEL, IN_COLS), D_MODEL ** -0.5),
        'conv_w': nrm(ks[14], (DEPTH, CONV_WIDTH, CONV_DIM), CONV_WIDTH ** -0.5),
        'ret_gn_g': 1.0 + nrm(ks[15], (DEPTH, RET_HEADS * RET_DV), 0.02),
        'diff_lambda': nrm(ks[16], (DEPTH, 4, DIFF_D), 0.1),
        'diff_subln_g': 1.0 + nrm(ks[17], (DEPTH, DIFF_DV), 0.02),
        'w_mem_kv': nrm(ks[18], (DEPTH, D_MODEL, 2 * MEM_HEADS * MEM_HD), D_MODEL ** -0.5),
        'w_branch': nrm(ks[19], (DEPTH, N_BRANCH, BRANCH_W, D_MODEL), beta * BRANCH_W ** -0.5),
        'w_gate': nrm(ks[20], (DEPTH, N_BRANCH, D_MODEL, D_MODEL), D_MODEL ** -0.5),
        'b_gate': nrm(ks[21], (DEPTH, N_BRANCH, D_MODEL), 0.02),
        'w_o': nrm(ks[22], (DEPTH, D_MODEL, D_MODEL), beta * D_MODEL ** -0.5),
        'ln2_g': 1.0 + nrm(ks[23], (DEPTH, D_MODEL), 0.02),
        'ln2_b': nrm(ks[24], (DEPTH, D_MODEL), 0.02),
        'ffn2_w_up': nrm(ks[25], (DEPTH, D_MODEL, 2 * D_FF), beta * D_MODEL ** -0.5),
        'ffn2_w_down': nrm(ks[26], (DEPTH, D_FF, D_MODEL), beta * D_FF ** -0.5),
        'ln3_g': 1.0 + nrm(ks[27], (DEPTH, D_MODEL), 0.02),
        'ln3_b': nrm(ks[28], (DEPTH, D_MODEL), 0.02),
        'rel_bias': nrm(ks[29], (REL_BUCKETS, DIFF_HEADS), 0.5),
    }


def reference(x_prompt, x_sample, state_conv, state_ret, cache_diff_k, cache_diff_v, cache_mem_k, cache_mem_v,
              mem_prompt, ffn1_w_up, ffn1_w_down, ln1_g, ln1_b, w_in, conv_w, ret_gn_g, diff_lambda,
              diff_subln_g, w_mem_kv, w_branch, w_gate, b_gate, w_o, ln2_g, ln2_b, ffn2_w_up, ffn2_w_down,
              ln3_g, ln3_b, rel_bias):
    pos_p = jnp.arange(SEQ, dtype=jnp.int32)
    pos_s = PAST_LEN + jnp.arange(DEC_SEQ, dtype=jnp.int32)
    past_pos = jnp.arange(PAST_LEN, dtype=jnp.int32)
    yp, ys = x_prompt, x_sample
    conv_p, ret_p, dk_p, dv_p, mk_p, mv_p = [], [], [], [], [], []
    conv_s, ret_s, dk_s, dv_s = [], [], [], []
    for l in range(DEPTH):
        lw = {
            'ffn1_w_up': ffn1_w_up[l], 'ffn1_w_down': ffn1_w_down[l], 'ln1_g': ln1_g[l], 'ln1_b': ln1_b[l],
            'w_in': w_in[l], 'conv_w': conv_w[l], 'ret_gn_g': ret_gn_g[l], 'diff_lambda': diff_lambda[l],
            'diff_subln_g': diff_subln_g[l], 'w_branch': w_branch[l], 'w_gate': w_gate[l], 'b_gate': b_gate[l],
            'w_o': w_o[l], 'ln2_g': ln2_g[l], 'ln2_b': ln2_b[l], 'ffn2_w_up': ffn2_w_up[l],
            'ffn2_w_down': ffn2_w_down[l], 'ln3_g': ln3_g[l], 'ln3_b': ln3_b[l],
        }
        mk, mv = memory_kv(mem_prompt, w_mem_kv[l])
        yp, (c_new, r_new, k_new, v_new) = encoder_layer(yp, pos_p, l, lw, rel_bias, None, None, None, None,
                                                         None, mk, mv)
        conv_p.append(c_new)
        ret_p.append(r_new)
        dk_p.append(k_new)
        dv_p.append(v_new)
        mk_p.append(mk)
        mv_p.append(mv)
        ys, (c_new, r_new, k_new, v_new) = encoder_layer(ys, pos_s, l, lw, rel_bias, state_conv[l], state_ret[l],
                                                         cache_diff_k[l], cache_diff_v[l], past_pos,
                                                         cache_mem_k[l], cache_mem_v[l])
        conv_s.append(c_new)
        ret_s.append(r_new)
        dk_s.append(k_new)
        dv_s.append(v_new)
    return (yp, ys, jnp.stack(conv_p), jnp.stack(ret_p), jnp.stack(dk_p), jnp.stack(dv_p), jnp.stack(mk_p),
            jnp.stack(mv_p), jnp.stack(conv_s), jnp.stack(ret_s), jnp.stack(dk_s), jnp.stack(dv_s))
```

```python
import contextlib
import numpy as np
import concourse.bass as bass
import concourse.mybir as mybir

F32 = mybir.dt.float32
BF16 = mybir.dt.bfloat16
AF = mybir.ActivationFunctionType
ALU = mybir.AluOpType
AX = mybir.AxisListType

COMPUTE = ("pe", "act", "dve", "pool")
EPOCH = 24000


class Tile:
    __slots__ = ("name", "t", "recs", "space")

    def __init__(self, name, t, space):
        self.name = name
        self.t = t
        self.space = space
        self.recs = []

    def __getitem__(self, idx):
        return self.t[idx]


class Op:
    __slots__ = ("eng", "idx", "fn", "waits", "needs_inc", "is_dma", "slot", "val", "ticket", "prewait")

    def __init__(self, eng, idx, fn, is_dma):
        self.eng = eng
        self.idx = idx
        self.fn = fn
        self.waits = []
        self.needs_inc = False
        self.is_dma = is_dma
        self.slot = None
        self.val = None
        self.ticket = None
        self.prewait = None


class Prog:
    def __init__(self, nc, n_dma_sems=None):
        self.nc = nc
        self.stack = contextlib.ExitStack()
        self.ops = {e: [] for e in ("pe", "act", "dve", "pool", "sp")}
        self.known = {e: {c: -1 for c in COMPUTE} for e in self.ops}
        self.known_dma = {e: set() for e in self.ops}
        self.ndma = {"sp": 0, "pool": 0, "act": 0}
        self.nslots = n_dma_sems or {"sp": 16, "pool": 8, "act": 4}
        self.dma_ops = {"sp": [], "pool": [], "act": []}
        self.tiles = []
        self.stacks = [self.stack]
        self.flushed = {e: 0 for e in self.ops}
        self.tk = {e: 0 for e in COMPUTE}
        self.lastt = {e: None for e in COMPUTE}
        import os as _os
        NEP = int(_os.environ.get('NEP', '8'))
        self.csem = {e: [self.stack.enter_context(nc.semaphore(f"s_{e}_{i}")) for i in range(NEP)] for e in COMPUTE}
        self.dsem = {q: [self.stack.enter_context(nc.semaphore(f"d_{q}_{i}")) for i in range(self.nslots[q])]
                     for q in ("sp", "pool")}
        self.nbank = 0
        self.banks = None

    def sb(self, name, shape, dtype):
        self.uid = getattr(self, "uid", 0) + 1
        t = self.stacks[-1].enter_context(self.nc.sbuf_tensor(f"sb{self.uid}_{name}", list(shape), dtype))
        tl = Tile(name, t, "sb")
        self.tiles.append(tl)
        return tl

    def ps(self, name, shape, dtype):
        t = self.stack.enter_context(self.nc.psum_tensor("ps_" + name, list(shape), dtype))
        tl = Tile(name, t, "ps")
        self.tiles.append(tl)
        return tl

    def dram(self, name, shape, dtype, kind):
        t = self.nc.dram_tensor(name, list(shape), dtype, kind=kind)
        tl = Tile(name, t.ap(), "dram")
        self.tiles.append(tl)
        return tl

    def _deps(self, reads, writes):
        deps = []
        for (tl, lo, hi) in reads:
            for r in tl.recs:
                if r[0] < hi and lo < r[1]:
                    if r[2] is not None:
                        deps.append((r[2], "raw"))
        for (tl, lo, hi) in writes:
            for r in tl.recs:
                if r[0] < hi and lo < r[1]:
                    if r[2] is not None:
                        deps.append((r[2], "waw"))
                    for o in r[3].values():
                        deps.append((o, "war"))
        return deps

    def _update(self, op, reads, writes):
        for (tl, lo, hi) in reads:
            covered = []
            for r in tl.recs:
                if r[0] < hi and lo < r[1]:
                    r[3][op.eng if not op.is_dma else ("dma", id(op))] = op
                    covered.append((r[0], r[1]))
            covered.sort()
            cur = lo
            newrecs = []
            for (a, b) in covered:
                if a > cur:
                    newrecs.append([cur, min(a, hi), None, {}])
                cur = max(cur, b)
            if cur < hi:
                newrecs.append([cur, hi, None, {}])
            for nr in newrecs:
                nr[3][op.eng if not op.is_dma else ("dma", id(op))] = op
                tl.recs.append(nr)
        for (tl, lo, hi) in writes:
            keep = []
            for r in tl.recs:
                if r[0] >= lo and r[1] <= hi:
                    continue
                if r[0] < hi and lo < r[1]:
                    if r[0] < lo:
                        keep.append([r[0], lo, r[2], dict(r[3])])
                    if r[1] > hi:
                        keep.append([hi, r[1], r[2], dict(r[3])])
                    continue
                keep.append(r)
            keep.append([lo, hi, op, {}])
            tl.recs = keep

    def add(self, eng, fn, reads=(), writes=(), dma=False):
        ops = self.ops[eng]
        op = Op(eng, len(ops), fn, dma)
        reads = [(t, 0, 1) if t.space == "ps" else (t, lo, hi) for (t, lo, hi) in reads]
        writes = [(t, 0, 1) if t.space == "ps" else (t, lo, hi) for (t, lo, hi) in writes]
        deps = self._deps(reads, writes)
        kn = self.known[eng]
        kd = self.known_dma[eng]
        for (d, kind) in deps:
            if d.is_dma:
                if id(d) in kd:
                    continue
                kd.add(id(d))
                op.waits.append(d)
            else:
                if d.eng == eng and not dma:
                    if eng == "pe":
                        continue
                if kn[d.eng] >= d.idx:
                    continue
                kn[d.eng] = d.idx
                d.needs_inc = True
                op.waits.append(d)
        if dma:
            k = self.ndma[eng]
            ns = self.nslots[eng]
            op.slot = k % ns
            op.val = 16 * (k // ns + 1)
            if k >= ns:
                prev = self.dma_ops[eng][k - ns]
                if id(prev) not in kd:
                    kd.add(id(prev))
                    op.waits.append(prev)
            self.ndma[eng] = k + 1
            self.dma_ops[eng].append(op)
        ops.append(op)
        self._update(op, reads, writes)
        return op

    @contextlib.contextmanager
    def scope(self):
        st = contextlib.ExitStack()
        self.stacks.append(st)
        try:
            yield
        finally:
            self.flush()
            self.stacks.pop()
            st.close()

    def flush(self, final=False):
        nc = self.nc
        csem, dsem = self.csem, self.dsem
        new = {e: self.ops[e][self.flushed[e]:] for e in self.ops}
        for e in COMPUTE:
            real = [op for op in new[e] if op.fn is not None and not op.is_dma]
            if real:
                real[-1].needs_inc = True
            for op in new[e]:
                if op.needs_inc and not op.is_dma and op.fn is not None:
                    op.ticket = self.tk[e]
                    self.tk[e] += 1
                    self.lastt[e] = op
            assert self.tk[e] < EPOCH * len(csem[e]), "too many tickets"

        def wait_for(engobj, d):
            if d.is_dma:
                engobj.wait_ge(dsem[d.eng][d.slot], d.val)
            else:
                engobj.wait_ge(csem[d.eng][d.ticket // EPOCH], d.ticket % EPOCH + 1)

        def run(ename):
            def body(engobj):
                for op in new[ename]:
                    for d in op.waits:
                        wait_for(engobj, d)
                    if op.fn is None:
                        continue
                    inst = op.fn(engobj)
                    if op.is_dma:
                        inst.then_inc(dsem[op.eng][op.slot], 16)
                    elif op.needs_inc:
                        inst.then_inc(csem[op.eng][op.ticket // EPOCH], 1)
                if final and ename == "sp":
                    for q, lst in self.dma_ops.items():
                        ns = self.nslots[q]
                        for op in lst[-ns:]:
                            engobj.wait_ge(dsem[q][op.slot], op.val)
                    for e in COMPUTE:
                        if self.lastt[e] is not None:
                            wait_for(engobj, self.lastt[e])
            return body

        if not hasattr(self, "pending"):
            self.pending = {e: [] for e in self.ops}
        for e in self.ops:
            self.pending[e].extend(new[e])
        if final:
            new = self.pending
            with nc.Block() as block:
                block.sync(run("sp"))
                block.tensor(run("pe"))
                block.scalar(run("act"))
                block.vector(run("dve"))
                block.gpsimd(run("pool"))
        for e in self.ops:
            self.flushed[e] = len(self.ops[e])
        if final:
            return
        for e in self.ops:
            op = Op(e, len(self.ops[e]), None, False)
            for c in COMPUTE:
                lt = self.lastt[c]
                if lt is not None and self.known[e][c] < lt.idx:
                    op.waits.append(lt)
                    self.known[e][c] = lt.idx
            for q, lst in self.dma_ops.items():
                for d in lst[-self.nslots[q]:]:
                    if id(d) not in self.known_dma[e]:
                        self.known_dma[e].add(id(d))
                        op.waits.append(d)
            self.ops[e].append(op)
        for tl in self.tiles:
            tl.recs = []

    def close(self):
        self.stack.close()


import math
import numpy as np
import ml_dtypes
import concourse.bass as bass
import concourse.mybir as mybir
from concourse.bass_utils import run_bass_kernel_spmd

D = 2048
DC = 16
DFF = 5632
FC = 44
BIGR = 1 << 30
NEG = -1e30
LN_EPS = 1e-5
BF = ml_dtypes.bfloat16


class Cfg:
    def __init__(self, SEQ=4096, PAST=2048, NS=2, DEPTH=2, T=512):
        self.SEQ, self.PAST, self.NS, self.DEPTH, self.T = SEQ, PAST, NS, DEPTH, T
        self.NT = SEQ // T
        self.ALPHA = (2 * DEPTH) ** 0.25


def V(t, ap=None, lo=0, hi=BIGR):
    return (t, t.t[:] if ap is None else ap, lo, hi)


def _r(v):
    return (v[0], v[2], v[3])


class KB:
    def __init__(self, cfg):
        self.cfg = cfg
        self.nc = bass.Bass("TRN2", target_bir_lowering=False)
        self.P = Prog(self.nc)
        self.bank_i = 0
        self.wslot_i = 0
        self.tmp_i = 0

    def mm(self, o, l, r, start=True, stop=True):
        self.P.add("pe", lambda e: e.matmul(o[1], lhsT=l[1], rhs=r[1], start=start, stop=stop),
                   reads=[_r(l), _r(r)], writes=[_r(o)])

    def act(self, o, i, func, bias=None, scale=None, accum=None):
        if i[0].space == "ps":
            shp = list(i[1].shape)
            assert len(shp) == 2, shp
            sc = self.tmp()
            sv = V(sc, sc[0:shp[0], 0:shp[1]])
            self.cp("dve", sv, i)
            i = sv
        reads = [_r(i)]
        kw = {}
        if bias is not None:
            if isinstance(bias, tuple):
                reads.append(_r(bias)); kw["bias"] = bias[1]
            else:
                kw["bias"] = float(bias)
        if scale is not None:
            if isinstance(scale, tuple):
                reads.append(_r(scale)); kw["scale"] = scale[1]
            else:
                kw["scale"] = float(scale)
        writes = [_r(o)]
        if accum is not None:
            kw["accum_out"] = accum[1]; writes.append(_r(accum))
        self.P.add("act", lambda e: e.activation(out=o[1], in_=i[1], func=func, **kw), reads=reads, writes=writes)

    def tt(self, eng, o, a, b, op):
        self.P.add(eng, lambda e: e.tensor_tensor(out=o[1], in0=a[1], in1=b[1], op=op), reads=[_r(a), _r(b)], writes=[_r(o)])

    def ts(self, eng, o, a, s1, s2=None, op0=ALU.mult, op1=None):
        reads = [_r(a)]
        a1 = s1
        if isinstance(s1, tuple):
            reads.append(_r(s1)); a1 = s1[1]
        a2 = s2
        if isinstance(s2, tuple):
            reads.append(_r(s2)); a2 = s2[1]
        kw = {} if op1 is None else {"op1": op1}
        self.P.add(eng, lambda e: e.tensor_scalar(out=o[1], in0=a[1], scalar1=a1, scalar2=a2, op0=op0, **kw), reads=reads, writes=[_r(o)])

    def stt(self, eng, o, a, s, b, op0, op1):
        reads = [_r(a), _r(b)]
        sc = s
        if isinstance(s, tuple):
            reads.append(_r(s)); sc = s[1]
        self.P.add(eng, lambda e: e.scalar_tensor_tensor(out=o[1], in0=a[1], scalar=sc, in1=b[1], op0=op0, op1=op1), reads=reads, writes=[_r(o)])

    def cp(self, eng, o, a):
        if eng == "act" and a[0].space == "ps":
            eng = "dve"
        if eng == "act":
            return self.act(o, a, AF.Copy)
        self.P.add(eng, lambda e: e.tensor_copy(out=o[1], in_=a[1]), reads=[_r(a)], writes=[_r(o)])

    def red(self, eng, o, a, op):
        self.P.add(eng, lambda e: e.tensor_reduce(out=o[1], in_=a[1], axis=AX.X, op=op), reads=[_r(a)], writes=[_r(o)])

    def recip(self, o, a):
        self.P.add("dve", lambda e: e.reciprocal(out=o[1], in_=a[1]), reads=[_r(a)], writes=[_r(o)])

    def memset(self, eng, o, val):
        self.P.add(eng, lambda e: e.memset(o[1], val), writes=[_r(o)])

    def dma(self, q, o, i, slow=False):
        kw = {"allow_slow_non_contiguous": True} if slow else {}
        self.P.add(q, lambda e: e.dma_start(out=o[1], in_=i[1], **kw), reads=[_r(i)], writes=[_r(o)], dma=True)

    def bank(self):
        b = self.banks[self.bank_i % 6]
        self.bank_i += 1
        return b

    def bankO(self):
        self.banko_i = getattr(self, "banko_i", 0) + 1
        return self.banks[6 + self.banko_i % 2]

    def tmp(self):
        t = self.tmps[self.tmp_i % len(self.tmps)]
        self.tmp_i += 1
        return t

    def wload(self, wt, lidx, row0, kc, col0, ncols=512, prefetch=False, cache=True):
        pf = self.__dict__.setdefault("pf", {})
        key = (wt.name, lidx, row0, kc, col0, ncols)
        if not prefetch and key in pf:
            return pf.pop(key)
        s = self.wslot_i % self.NBUF
        self.wslot_i += 1
        for k_ in [k_ for k_, v_ in pf.items() if v_ == s]:
            del pf[k_]
        if prefetch:
            pf[key] = s
        uids = self.__dict__.setdefault("unit_ids", {})
        if cache and key in uids and self.tix > 0:
            u = uids[key]
            wscr = self.wscrs[u // 64]
            src = wscr.t[u % 64, :, 0:kc * ncols].rearrange("p (k n) -> p k n", k=kc)
            self.dma("sp", V(self.WR, self.WR[:, s, 0:kc, 0:ncols], s, s + 1), V(wscr, src, u % 64, u % 64 + 1))
            return s
        src = wt.t[lidx, row0:row0 + kc * 128, col0:col0 + ncols].rearrange("(k p) n -> p k n", p=128)
        self.dma("pool", V(self.WR, self.WR[:, s, 0:kc, 0:ncols], s, s + 1), V(wt, src))
        if cache and self.tix == 0 and key not in uids and len(uids) < self.NUNITS:
            u = uids[key] = len(uids)
            wscr = self.wscrs[u // 64]
            dst = wscr.t[u % 64, :, 0:kc * ncols].rearrange("p (k n) -> p k n", k=kc)
            self.dma("sp", V(wscr, dst, u % 64, u % 64 + 1), V(self.WR, self.WR[:, s, 0:kc, 0:ncols], s, s + 1))
        return s

    def prefetch(self, units):
        for u in (units or [])[:self.NBUF]:
            self.wload(*u, prefetch=True)

    def W(self, s, k, c0, c1):
        return V(self.WR, self.WR[:, s, k, c0:c1], s, s + 1)

    def build(self):
        cfg, P = self.cfg, self.P
        SEQ, PAST, NS, DEPTH, T, NT = cfg.SEQ, cfg.PAST, cfg.NS, cfg.DEPTH, cfg.T, cfg.NT
        TS = 32 * NS
        NKB = max(SEQ, PAST + 128) // 128
        L = DEPTH
        dr = P.dram
        I, O = "ExternalInput", "ExternalOutput"
        self.xp = dr("xp", [SEQ, D], F32, I)
        self.xs = dr("xs", [NS * 16, D], F32, I)
        self.memp = dr("memp", [256, D], F32, I)
        self.st_conv = dr("st_conv", [L, NS, 2, 512], F32, I)
        self.st_ret = dr("st_ret", [L, NS, 4, 128, 128], F32, I)
        self.c_k = dr("c_k", [L, NS, PAST, 512], F32, I)
        self.c_v = dr("c_v", [L, NS, PAST, 512], F32, I)
        self.c_mk = dr("c_mk", [L, NS, 256, 512], F32, I)
        self.c_mv = dr("c_mv", [L, NS, 256, 512], F32, I)
        self.w_up1 = dr("w_up1", [L, D, 2 * DFF], F32, I)
        self.w_dn1 = dr("w_dn1", [L, DFF, D], F32, I)
        self.w_up2 = dr("w_up2", [L, D, 2 * DFF], F32, I)
        self.w_dn2 = dr("w_dn2", [L, DFF, D], F32, I)
        self.w_in = dr("w_in", [L, D, 13 * 512], F32, I)
        self.w_mkv = dr("w_mkv", [L, D, 1024], F32, I)
        self.w_br = dr("w_br", [L * 4, 512, D], F32, I)
        self.w_gt = dr("w_gt", [L * 4, D, D], F32, I)
        self.w_o = dr("w_o", [L, D, D], F32, I)
        self.lnp_d = dr("lnp", [128, L * 6 * 16], F32, I)
        self.bg_d = dr("bgate", [128, L * 4 * 16], F32, I)
        self.cw_d = dr("convw", [128, L * 3 * 4], F32, I)
        self.gn_d = dr("gn", [L, 512], F32, I)
        self.sub_d = dr("subg", [L, 128], F32, I)
        self.lam_d = dr("lam", [L, 256], F32, I)
        self.rb_d = dr("relb", [1, 128], F32, I)
        self.rb2_d = dr("relb2", [32, 4], F32, I)
        self.rope_d = dr("rope", [NT + 1, 2, 128, T], F32, I)
        self.qd_d = dr("qd", [2, 128, 4 * T], BF16, I)
        self.rm_d = dr("rm", [2, 128, 512], F32, I)
        self.kdx_d = dr("kdx", [2, 128, 512], F32, I)
        self.cdec_d = dr("cdec", [2, 128, 4], F32, I)
        self.oh_d = dr("oh", [32, 2 * 128 * 128], F32, I)
        self.mask_d = dr("mask0", [128, 128], F32, I)
        self.id_d = dr("ident", [128, 128], F32, I)
        self.idb_d = dr("identb", [128, 128], BF16, I)
        self.on_d = dr("ones", [128, 128], BF16, I)
        self.NUNITS = 128 * L
        self.wscrs = [dr(f"wscr{i}", [64, 128, 16 * 512], BF16, "Internal") for i in range(self.NUNITS // 64)]
        self.tix = 0
        self.yp = dr("yp", [SEQ, D], F32, O)
        self.ys = dr("ys", [NS * 16, D], F32, O)
        self.o_conv_p = dr("o_conv_p", [L, 2, 512], F32, O)
        self.o_ret_p = dr("o_ret_p", [L, 4, 128, 128], F32, O)
        self.o_dk_p = dr("o_dk_p", [L, SEQ, 512], F32, O)
        self.o_dv_p = dr("o_dv_p", [L, SEQ, 512], F32, O)
        self.o_mk_p = dr("o_mk_p", [L, 256, 512], F32, O)
        self.o_mv_p = dr("o_mv_p", [L, 256, 512], F32, O)
        self.o_conv_s = dr("o_conv_s", [L, NS, 2, 512], F32, O)
        self.o_ret_s = dr("o_ret_s", [L, NS, 4, 128, 128], F32, O)
        self.o_dk_s = dr("o_dk_s", [L, NS, 16, 512], F32, O)
        self.o_dv_s = dr("o_dv_s", [L, NS, 16, 512], F32, O)
        import os as _os
        self.dbg = dr("dbg", [128, 4 * 64], BF16, O) if _os.environ.get("K_DBG") else None

        sb = P.sb
        self.banks = [P.ps(f"bank{i}", [128, 512], F32) for i in range(8)]
        self.NBUF = 3
        self.WR = sb("WR", [128, self.NBUF, 16, 512], BF16)
        self.X = sb("X", [128, DC, T], F32)
        self.X16 = sb("X16", [128, DC, T], BF16)
        self.tmps = [sb(f"tmp{i}", [128, 512], F32) for i in range(4)]

        self.YT = [sb(f"YT{b}", [128, 4, T], BF16) for b in range(4)]
        self.DQ = sb("DQ", [128, 4, T], BF16)
        self.MQ = sb("MQ", [128, 4, T], BF16)
        self.UC = [sb(f"UC{l}", [128, 4, 2], F32) for l in range(L)]
        self.SL = [sb(f"SL{l}", [128, 4, 128], F32) for l in range(L)]
        self.MKT = [sb(f"MKT{l}", [128, 4, 256], BF16) for l in range(L)]
        self.MEMV = [sb(f"MEMV{l}", [128, 2, 512], BF16) for l in range(L)]
        self.ident = sb("ident", [128, 128], F32)
        self.identb = sb("identb", [128, 128], BF16)
        self.ones = sb("ones", [128, 128], BF16)
        self.lnp = sb("lnp", [128, L * 6 * 16], F32)
        self.bg = sb("bg", [128, L * 4 * 16], F32)
        self.cw = sb("cw", [128, L * 3 * 4], F32)
        self.GN = sb("GN", [128, L, 512], BF16)
        self.SUBG = sb("SUBG", [128, L, 128], F32)
        self.LAMC = sb("LAMC", [128, L, 4], F32)
        self.BT = sb("BT", [128, 2, 4, 128], F32)
        self.CF = sb("CF", [128, 8], F32)
        self.ROPE = sb("ROPE", [128, 2, T], F32)
        self.QD = sb("QD", [128, 4, T], BF16)
        self.RM = sb("RM", [128, 4, 128], F32)
        self.KDX = sb("KDX", [128, 512], F32)
        self.CDEC = sb("CDEC", [128, 4], F32)
        self.stat = sb("stat", [128, 64], F32)
        self.S16 = [sb(f"S16_{i}", [128, 128], BF16) for i in range(2)]
        self.s16_i = 0

        q = "sp"
        for (dst, src) in ((self.ident, self.id_d), (self.identb, self.idb_d), (self.ones, self.on_d), (self.lnp, self.lnp_d),
                           (self.bg, self.bg_d), (self.cw, self.cw_d)):
            self.dma(q, V(dst), V(src))
        for l in range(L):
            self.dma("pool", V(self.GN, self.GN[:, l, :]), V(self.gn_d, self.gn_d[l:l + 1, :].partition_broadcast(128)))
            self.dma(q, V(self.SUBG, self.SUBG[:, l, :]), V(self.sub_d, self.sub_d[l:l + 1, :].partition_broadcast(128)))
        import os as _os
        STG0 = int(_os.environ.get("K_STAGE", "9"))
        if STG0 != -1:
            self.setup_consts()
        P.flush()

        KT_ = _os.environ.get("K_TILES", "ps")
        for t in range(NT + 1 if STG0 >= 0 else 0):
            sample = (t == NT)
            if sample and "s" not in KT_:
                continue
            if not sample and "p" not in KT_:
                continue
            if not sample and "1" in KT_ and t > 0:
                continue
            Tt = TS if sample else T
            self.tile_T = Tt
            self.sample = sample
            self.tix = t
            self.tbs = [(0, TS)] if sample else [(i * 128, 128) for i in range(T // 128)]
            self.load_tile(t)
            import os as _os
            STG = int(_os.environ.get("K_STAGE", "9"))
            for l in range(L):
                if t == 0 and STG >= 1:
                    self.mem_kv(l)
                wi = self.w_in
                up2 = self.w_up2
                n_mix = [(wi, l, 0, 16, 0), (wi, l, 0, 16, 512), (wi, l, 0, 16, 1024)]
                n_f2 = [(up2, l, 0, 16, 0), (up2, l, 0, 16, 512), (up2, l, 0, 16, 1024)]
                if l + 1 < L:
                    n_f1 = None if t == 0 else [(self.w_up1, l + 1, 0, 16, 0), (self.w_up1, l + 1, 0, 16, 512), (self.w_up1, l + 1, 0, 16, 1024)]
                else:
                    n_f1 = None
                if STG >= 2:
                    self.ffn(l, 0, n_mix)
                if STG >= 3:
                    self.mixer(l)
                if STG >= 6:
                    self.gates(l, n_f2)
                if STG >= 7:
                    self.ffn(l, 1, n_f1)
            if "n" not in KT_:
                self.store_tile(t)
        P.flush(final=True)
        P.close()
        return self.nc

    def setup_consts(self):
        P, L = self.P, self.cfg.DEPTH
        with P.scope():
            ohs = [P.sb(f"oh{i}", [32, 16 * 128], F32) for i in range(2)]
            rb2 = P.sb("rb2", [32, 4], F32)
            rbb = P.sb("rbb", [128, 128], F32)
            msk = P.sb("msk", [128, 128], F32)
            lamb = P.sb("lamb", [128, L, 256], F32)
            lt = P.sb("lt", [128, 256], F32)
            self.dma("sp", V(rb2), V(self.rb2_d))
            self.dma("sp", V(rbb), V(self.rb_d, self.rb_d[0:1, :].partition_broadcast(128)))
            self.dma("sp", V(msk), V(self.mask_d))
            for l in range(L):
                self.dma("sp", V(lamb, lamb[:, l, :]), V(self.lam_d, self.lam_d[l:l + 1, :].partition_broadcast(128)))
            for ti in range(2):
                bk = self.bank()
                for kg in range(8):
                    oh = ohs[kg % 2]
                    o0 = (ti * 128 + kg * 16) * 128
                    self.dma("sp", V(oh), V(self.oh_d, self.oh_d[:, o0:o0 + 16 * 128]))
                    for kk in range(16):
                        k = kg * 16 + kk
                        self.mm(V(bk, bk[:, k * 4:(k + 1) * 4]), V(oh, oh[:, kk * 128:(kk + 1) * 128]), V(rb2), True, True)
                for h in range(4):
                    src = bk[:].rearrange("p (k h) -> p h k", h=4)[:, h, :]
                    if ti == 0:
                        self.tt("dve", V(self.BT, self.BT[:, ti, h, :]), V(bk, src), V(msk), ALU.add)
                    else:
                        self.cp("dve", V(self.BT, self.BT[:, ti, h, :]), V(bk, src))
            self.cp("dve", V(self.CF, self.CF[:, 0:4]), V(rbb, rbb[:, 60:64]))
            self.red("dve", V(self.CF, self.CF[:, 4:8]), V(rbb, rbb[:].rearrange("p (b h) -> p h b", h=4)), ALU.max)
            for l in range(L):
                lam_init = 0.8 - 0.6 * math.exp(-0.3 * l)
                self.tt("dve", V(lt, lt[:, 0:64]), V(lamb, lamb[:, l, 0:64]), V(lamb, lamb[:, l, 64:128]), ALU.mult)
                self.tt("dve", V(lt, lt[:, 64:128]), V(lamb, lamb[:, l, 128:192]), V(lamb, lamb[:, l, 192:256]), ALU.mult)
                self.red("dve", V(self.stat, self.stat[:, 0:1]), V(lt, lt[:, 0:64]), ALU.add)
                self.red("dve", V(self.stat, self.stat[:, 1:2]), V(lt, lt[:, 64:128]), ALU.add)
                self.act(V(self.stat, self.stat[:, 2:4]), V(self.stat, self.stat[:, 0:2]), AF.Exp)
                self.tt("dve", V(self.stat, self.stat[:, 4:5]), V(self.stat, self.stat[:, 2:3]), V(self.stat, self.stat[:, 3:4]), ALU.subtract)
                self.ts("dve", V(self.LAMC, self.LAMC[:, l, 0:1]), V(self.stat, self.stat[:, 4:5]), lam_init, None, ALU.add)
                self.ts("dve", V(self.LAMC, self.LAMC[:, l, 1:2]), V(self.LAMC, self.LAMC[:, l, 0:1]), -1.0, None, ALU.mult)

    def load_tile(self, t):
        P, cfg = self.P, self.cfg
        T, NS = cfg.T, cfg.NS
        kind = 1 if self.sample else 0
        Tt = self.tile_T
        with P.scope():
            stg = P.sb("stg", [128, D], F32)
            import os as _os
            SK = _os.environ.get("K_SKIP", "")
            if "r" not in SK:
                self.dma("sp", V(self.ROPE), V(self.rope_d, self.rope_d[t].rearrange("a p t -> p a t")))
            if "q" not in SK:
                self.dma("sp", V(self.QD), V(self.qd_d, self.qd_d[kind].rearrange("p (h t) -> p h t", h=4)))
                self.dma("sp", V(self.RM), V(self.rm_d, self.rm_d[kind].rearrange("p (h t) -> p h t", h=4)))
            if "k" not in SK:
                self.dma("sp", V(self.KDX), V(self.kdx_d, self.kdx_d[kind]))
                self.dma("sp", V(self.CDEC), V(self.cdec_d, self.cdec_d[kind]))
            for (c0, nr) in (self.tbs if "x" not in SK else []):
                if self.sample:
                    self.memset("dve", V(stg), 0.0)
                    for s in range(NS):
                        self.dma("sp", V(stg, stg[32 * s:32 * s + 16, :]), V(self.xs, self.xs[16 * s:16 * s + 16, :]))
                else:
                    self.dma("sp", V(stg), V(self.xp, self.xp[t * T + c0:t * T + c0 + nr, :]))
                for g in range(4):
                    bk = self.bank()
                    for j in range(4):
                        c = g * 4 + j
                        self.mm(V(bk, bk[:, j * 128:j * 128 + nr]), V(stg, stg[0:nr, c * 128:(c + 1) * 128]),
                                V(self.ident, self.ident[0:nr, 0:nr]))
                    for j in range(4):
                        c = g * 4 + j
                        if "d" not in SK:
                            self.cp("dve", V(self.X, self.X[:, c, c0:c0 + nr], c, c + 1), V(bk, bk[:, j * 128:j * 128 + nr]))
                        if "a" not in SK:
                            self.cp("dve", V(self.X16, self.X16[:, c, c0:c0 + nr], c, c + 1), V(self.X, self.X[:, c, c0:c0 + nr], c, c + 1))

    def store_tile(self, t):
        P, cfg = self.P, self.cfg
        T, NS = cfg.T, cfg.NS
        with P.scope():
            stg = P.sb("stgo", [128, D], F32)
            for (c0, nr) in self.tbs:
                for g in range(4):
                    bk = self.bank()
                    for j in range(4):
                        c = g * 4 + j
                        self.mm(V(bk, bk[0:nr, j * 128:(j + 1) * 128]), V(self.X, self.X[:, c, c0:c0 + nr], c, c + 1), V(self.ident))
                    self.cp("act" if g % 2 else "dve", V(stg, stg[0:nr, g * 512:(g + 1) * 512], g, g + 1), V(bk, bk[0:nr, :]))
                if self.sample:
                    for s in range(NS):
                        self.dma("sp", V(self.ys, self.ys[16 * s:16 * s + 16, :]), V(stg, stg[32 * s:32 * s + 16, :]))
                else:
                    self.dma("sp", V(self.yp, self.yp[t * T + c0:t * T + c0 + nr, :]), V(stg, stg[0:nr, :]))

    def layernorm(self, l, which, SQ):
        Tt = self.tile_T
        X, X16 = self.X, self.X16
        for c in range(DC):
            self.cp("act", V(X16, X16[:, c, 0:Tt], c, c + 1), V(X, X[:, c, 0:Tt], c, c + 1))
            self.tt("pool", V(SQ, SQ[:, c, 0:Tt], c, c + 1), V(X, X[:, c, 0:Tt], c, c + 1), V(X, X[:, c, 0:Tt], c, c + 1), ALU.mult)
        bk = self.bank()
        bk2 = self.bank()
        for c in range(DC):
            self.mm(V(bk, bk[:, 0:Tt]), V(self.ones), V(X16, X16[:, c, 0:Tt], c, c + 1), c == 0, c == DC - 1)
        for c in range(DC):
            self.mm(V(bk2, bk2[:, 0:Tt]), V(self.ones), V(SQ, SQ[:, c, 0:Tt], c, c + 1), c == 0, c == DC - 1)
        t1, t2, t3, t4 = self.tmp(), self.tmp(), self.tmp(), self.tmp()
        mean = V(t1, t1[:, 0:Tt])
        self.cp("dve", mean, V(bk, bk[:, 0:Tt]))
        self.tt("dve", V(t2, t2[:, 0:Tt]), mean, mean, ALU.mult)
        self.tt("dve", V(t2, t2[:, 0:Tt]), V(bk2, bk2[:, 0:Tt]), V(t2, t2[:, 0:Tt]), ALU.subtract)
        self.ts("dve", V(t2, t2[:, 0:Tt]), V(t2, t2[:, 0:Tt]), 0.0, None, ALU.max)
        self.act(V(t2, t2[:, 0:Tt]), V(t2, t2[:, 0:Tt]), AF.Sqrt, bias=LN_EPS, scale=1.0)
        rstd = V(t3, t3[:, 0:Tt])
        self.recip(rstd, V(t2, t2[:, 0:Tt]))
        nmr = V(t4, t4[:, 0:Tt])
        self.stt("dve", nmr, mean, -1.0, rstd, ALU.mult, ALU.mult)
        gi = (l * 6 + 2 * which) * 16
        bi = (l * 6 + 2 * which + 1) * 16
        for c in range(DC):
            xc = V(X, X[:, c, 0:Tt], c, c + 1)
            self.tt("dve", xc, xc, rstd, ALU.mult)
            self.tt("pool", xc, xc, nmr, ALU.add)
            self.act(xc, xc, AF.Identity, bias=V(self.lnp, self.lnp[:, bi + c:bi + c + 1]), scale=V(self.lnp, self.lnp[:, gi + c:gi + c + 1]))
            self.cp("act", V(X16, X16[:, c, 0:Tt], c, c + 1), xc)

    def ffn(self, l, which, nxt=None):
        P = self.P
        Tt = self.tile_T
        wu = self.w_up1 if which == 0 else self.w_up2
        wd = self.w_dn1 if which == 0 else self.w_dn2
        X, X16 = self.X, self.X16
        with P.scope():
            HID = P.sb("HID", [128, FC, self.cfg.T], BF16)
            for g in range(22):
                sa = self.wload(wu, l, 0, 16, g * 512)
                for j in range(2):
                    bk = self.bank()
                    bkb = self.bank()
                    for k in range(DC):
                        self.mm(V(bk, bk[:, 0:Tt]), self.W(sa, k, j * 128, (j + 1) * 128), V(X16, X16[:, k, 0:Tt], k, k + 1), k == 0, k == DC - 1)
                    for k in range(DC):
                        self.mm(V(bkb, bkb[:, 0:Tt]), self.W(sa, k, 256 + j * 128, 256 + (j + 1) * 128), V(X16, X16[:, k, 0:Tt], k, k + 1), k == 0, k == DC - 1)
                    tm = self.tmp()
                    self.act(V(tm, tm[:, 0:Tt]), V(bk, bk[:, 0:Tt]), AF.Silu)
                    f = g * 2 + j
                    self.stt("dve", V(HID, HID[:, f, 0:Tt], f, f + 1), V(tm, tm[:, 0:Tt]), 0.5, V(bkb, bkb[:, 0:Tt]), ALU.mult, ALU.mult)
            for cg in range(4):
                bks = [self.bank(), self.bank(), self.bank(), self.bank()]
                for sub in range(4):
                    s = self.wload(wd, l, sub * 11 * 128, 11, cg * 512)
                    for oc in range(4):
                        bk = bks[oc]
                        o = V(bk, bk[:, 0:Tt])
                        for k in range(11):
                            f = sub * 11 + k
                            self.mm(o, self.W(s, k, oc * 128, (oc + 1) * 128), V(HID, HID[:, f, 0:Tt], f, f + 1),
                                    sub == 0 and k == 0, sub == 3 and k == 10)
                for oc in range(4):
                    c = cg * 4 + oc
                    bk = bks[oc]
                    xc = V(X, X[:, c, 0:Tt], c, c + 1)
                    self.stt("dve", xc, xc, self.cfg.ALPHA, V(bk, bk[:, 0:Tt]), ALU.mult, ALU.add)
            self.prefetch(nxt)
            self.layernorm(l, 0 if which == 0 else 2, HID)

    def mem_kv(self, l):
        P = self.P
        with P.scope():
            stg = P.sb("mstg", [128, D], F32)
            memT = P.sb("memT", [128, DC, 256], BF16)
            kf = P.sb("mkf", [128, 512], F32)
            for tb in range(2):
                self.dma("sp", V(stg), V(self.memp, self.memp[tb * 128:(tb + 1) * 128, :]))
                for g in range(4):
                    bk = self.bank()
                    for j in range(4):
                        c = g * 4 + j
                        self.mm(V(bk, bk[:, j * 128:(j + 1) * 128]), V(stg, stg[:, c * 128:(c + 1) * 128]), V(self.ident))
                    self.cp("act", V(memT, memT[:, g * 4:g * 4 + 4, tb * 128:(tb + 1) * 128]), V(bk, bk[:].rearrange("p (j n) -> p j n", j=4)))
            for kv in range(2):
                s = self.wload(self.w_mkv, l, 0, 16, kv * 512, cache=False)
                for tb in range(2):
                    bk = self.bank()
                    for k in range(DC):
                        self.mm(V(bk), V(memT, memT[:, k, tb * 128:(tb + 1) * 128]), self.W(s, k, 0, 512), k == 0, k == DC - 1)
                    self.cp("act", V(kf), V(bk))
                    dst = self.o_mk_p if kv == 0 else self.o_mv_p
                    self.dma("sp", V(dst, dst[l, tb * 128:(tb + 1) * 128, :]), V(kf))
                    if kv == 1:
                        self.cp("dve", V(self.MEMV[l], self.MEMV[l][:, tb, :]), V(bk))
                if kv == 0:
                    for hp in range(2):
                        bk = self.bank()
                        for hh in range(2):
                            h = hp * 2 + hh
                            for k in range(DC):
                                self.mm(V(bk, bk[:, hh * 256:(hh + 1) * 256]), self.W(s, k, h * 128, (h + 1) * 128), V(memT, memT[:, k, :]), k == 0, k == DC - 1)
                        self.cp("act", V(self.MKT[l], self.MKT[l][:, hp * 2:hp * 2 + 2, :]), V(bk, bk[:].rearrange("p (a n) -> p a n", a=2)))

    def proj_fm(self, s, j, bkv):
        Tt = self.tile_T
        for k in range(DC):
            self.mm(bkv, self.W(s, k, j * 128, (j + 1) * 128), V(self.X16, self.X16[:, k, 0:Tt], k, k + 1), k == 0, k == DC - 1)

    def proj_tm(self, s, c0, nr, bk):
        for k in range(DC):
            self.mm(V(bk, bk[0:nr, :]), V(self.X16, self.X16[:, k, c0:c0 + nr], k, k + 1), self.W(s, k, 0, 512), k == 0, k == DC - 1)

    def transpose_to_fm(self, src, dstT):
        for bi, (c0, nr) in enumerate(self.tbs):
            bk = self.bank()
            for h in range(4):
                self.mm(V(bk, bk[:, h * 128:h * 128 + nr]), V(src, src[0:nr, bi, h * 128:(h + 1) * 128]), V(self.identb, self.identb[0:nr, 0:nr]))
            self.cp("act", V(dstT, dstT[:, :, c0:c0 + nr]), V(bk, bk[:].rearrange("p (h n) -> p h n", h=4)[:, :, 0:nr]))

    def mixer(self, l):
        P, cfg = self.P, self.cfg
        T, NS, PAST, SEQ = cfg.T, cfg.NS, cfg.PAST, cfg.SEQ
        Tt = self.tile_T
        t = self.tix
        sample = self.sample
        ntb = len(self.tbs)
        X16 = self.X16
        win = self.w_in
        YAT, YBT, YCT, YDT = self.YT
        with P.scope():
            sb = P.sb
            CB = sb("CB", [128, 4, T], BF16)
            CC = sb("CC", [128, 4, T], F32)
            st = self.stat
            if sample:
                U = sb("US", [128, 4, NS * 18], F32)
                segs = [(32 * s, 16, 18 * s) for s in range(NS)]
                for s in range(NS):
                    for j in range(4):
                        self.dma("sp", V(U, U[:, j, 18 * s:18 * s + 2]),
                                 V(self.st_conv, self.st_conv[l, s, :, j * 128:(j + 1) * 128].rearrange("t p -> p t")), slow=True)
            else:
                U = sb("UP", [128, 4, 2 + T], F32)
                segs = [(0, T, 0)]
                if t == 0:
                    self.memset("dve", V(U, U[:, :, 0:2]), 0.0)
                else:
                    self.cp("dve", V(U, U[:, :, 0:2]), V(self.UC[l]))
            s0 = self.wload(win, l, 0, 16, 0)
            for j in range(4):
                bk = self.bank()
                self.proj_fm(s0, j, V(bk, bk[:, 0:Tt]))
                self.cp("dve", V(CB, CB[:, j, 0:Tt], j, j + 1), V(bk, bk[:, 0:Tt]))
            s1 = self.wload(win, l, 0, 16, 512)
            for j in range(4):
                bk = self.bank()
                self.proj_fm(s1, j, V(bk, bk[:, 0:Tt]))
                self.cp("dve", V(CC, CC[:, j, 0:Tt], j, j + 1), V(bk, bk[:, 0:Tt]))
            s2 = self.wload(win, l, 0, 16, 1024)
            for j in range(4):
                bk = self.bank()
                self.proj_fm(s2, j, V(bk, bk[:, 0:Tt]))
                for (c0, Ls, u0) in segs:
                    self.tt("dve", V(U, U[:, j, u0 + 2:u0 + 2 + Ls]), V(CC, CC[:, j, c0:c0 + Ls], j, j + 1), V(bk, bk[:, c0:c0 + Ls]), ALU.mult)
            for j in range(4):
                for (c0, Ls, u0) in segs:
                    tm = self.tmp()
                    z = V(tm, tm[:, 0:Ls])
                    wi = (l * 3) * 4
                    self.ts("dve", z, V(U, U[:, j, u0:u0 + Ls]), V(self.cw, self.cw[:, wi + j:wi + j + 1]), None, ALU.mult)
                    self.stt("dve", z, V(U, U[:, j, u0 + 1:u0 + 1 + Ls]), V(self.cw, self.cw[:, wi + 4 + j:wi + 4 + j + 1]), z, ALU.mult, ALU.add)
                    self.stt("dve", z, V(U, U[:, j, u0 + 2:u0 + 2 + Ls]), V(self.cw, self.cw[:, wi + 8 + j:wi + 8 + j + 1]), z, ALU.mult, ALU.add)
                    self.tt("dve", V(YAT, YAT[:, j, c0:c0 + Ls]), z, V(CB, CB[:, j, c0:c0 + Ls]), ALU.mult)
            if sample:
                self.memset("dve", V(YAT, YAT[:, :, 16:32]), 0.0)
                if NS > 1:
                    self.memset("dve", V(YAT, YAT[:, :, 48:64]), 0.0)
            if sample:
                for s in range(NS):
                    for j in range(4):
                        self.dma("sp", V(self.o_conv_s, self.o_conv_s[l, s, :, j * 128:(j + 1) * 128].rearrange("t p -> p t")),
                                 V(U, U[:, j, 18 * s + 16:18 * s + 18]), slow=True)
            else:
                if t == cfg.NT - 1:
                    for j in range(4):
                        self.dma("sp", V(self.o_conv_p, self.o_conv_p[l, :, j * 128:(j + 1) * 128].rearrange("t p -> p t")),
                                 V(U, U[:, j, T:T + 2]), slow=True)
                else:
                    self.cp("dve", V(self.UC[l]), V(U, U[:, :, T:T + 2]))
            self.prefetch([(win, l, 0, 16, 3 * 512), (win, l, 0, 16, 4 * 512), (win, l, 0, 16, 5 * 512)])
        with P.scope():
            sb = P.sb
            QT = sb("QT", [128, 4, T], BF16)
            KT = sb("KT", [128, 4, T], BF16)
            KH = sb("KH", [128, ntb, 512], BF16)
            VR = sb("VR", [128, ntb, 512], BF16)
            SG = sb("SG", [128, ntb, 512], BF16)
            ATM = sb("ATM", [128, 4, 128], BF16)
            OS = sb("OS", [128, 512], F32)
            YTOK = sb("YTOK", [128, ntb, 512], BF16)
            st = self.stat
            for (ua, ub, dst, isq) in ((3, 4, QT, True), (5, 6, KT, False)):
                sa = self.wload(win, l, 0, 16, ua * 512)
                sw = self.wload(win, l, 0, 16, ub * 512)
                for h in range(4):
                    bka, bkb = self.bank(), self.bank()
                    self.proj_fm(sa, h, V(bka, bka[:, 0:Tt]))
                    self.proj_fm(sw, h, V(bkb, bkb[:, 0:Tt]))
                    if True:
                        t1, t2 = self.tmp(), self.tmp()
                        a = V(t1, t1[:, 0:Tt]); b = V(t2, t2[:, 0:Tt])
                        self.tt("dve", a, V(bka, bka[:, 0:Tt]), V(self.ROPE, self.ROPE[:, 0, 0:Tt]), ALU.mult)
                        self.tt("dve", b, V(bkb, bkb[:, 0:Tt]), V(self.ROPE, self.ROPE[:, 1, 0:Tt]), ALU.mult)
                        if isq:
                            self.tt("dve", a, a, b, ALU.add)
                            self.tt("dve", V(dst, dst[:, h, 0:Tt]), a, V(self.QD, self.QD[:, h, 0:Tt]), ALU.mult)
                        else:
                            self.tt("dve", V(dst, dst[:, h, 0:Tt]), a, b, ALU.add)
            for bi, (c0, nr) in enumerate(self.tbs):
                bk = self.bank()
                for h in range(4):
                    self.mm(V(bk, bk[0:nr, h * 128:(h + 1) * 128]), V(KT, KT[:, h, c0:c0 + nr]), V(self.identb))
                self.tt("dve", V(KH, KH[0:nr, bi, :]), V(bk, bk[0:nr, :]), V(self.KDX, self.KDX[0:nr, :]), ALU.mult)
            s7 = self.wload(win, l, 0, 16, 7 * 512)
            for bi, (c0, nr) in enumerate(self.tbs):
                bk = self.bank()
                self.proj_tm(s7, c0, nr, bk)
                self.cp("act", V(VR, VR[0:nr, bi, :]), V(bk, bk[0:nr, :]))
            s8 = self.wload(win, l, 0, 16, 8 * 512)
            for bi, (c0, nr) in enumerate(self.tbs):
                bk = self.bank()
                self.proj_tm(s8, c0, nr, bk)
                self.act(V(SG, SG[0:nr, bi, :]), V(bk, bk[0:nr, :]), AF.Silu)
            if sample:
                SS = [sb(f"SS{s}", [128, 4, 128], F32) for s in range(NS)]
                for s in range(NS):
                    self.dma("sp", V(SS[s]), V(self.st_ret, self.st_ret[l, s].rearrange("h d e -> d h e")))
            else:
                if t == 0:
                    self.memset("dve", V(self.SL[l]), 0.0)
            for bi, (c0, nr) in enumerate(self.tbs):
                if sample:
                    chunks = [(32 * s, 32, 32 * s, SS[s]) for s in range(NS)]
                else:
                    chunks = [(0, 64, c0, self.SL[l]), (64, 64, c0 + 64, self.SL[l])]
                bkA = self.bank()
                for h in range(4):
                    self.mm(V(bkA, bkA[0:nr, h * 128:h * 128 + nr]), V(KT, KT[:, h, c0:c0 + nr]), V(QT, QT[:, h, c0:c0 + nr]))
                self.tt("dve", V(ATM, ATM[0:nr, :, 0:nr]), V(bkA, bkA[0:nr, :].rearrange("p (h n) -> p h n", h=4)[:, :, 0:nr]),
                        V(self.RM, self.RM[0:nr, :, 0:nr]), ALU.mult)
                bkO = self.bankO()
                for h in range(4):
                    for ci, (p0, ln, cc0, ST) in enumerate(chunks):
                        s16 = self.S16[self.s16_i % 2]; self.s16_i += 1
                        self.cp("act", V(s16), V(ST, ST[:, h, :], h, h + 1))
                        self.mm(V(bkO, bkO[p0:p0 + ln, h * 128:(h + 1) * 128]), V(ATM, ATM[0:nr, h, p0:p0 + ln]), V(VR, VR[0:nr, bi, h * 128:(h + 1) * 128]), True, False)
                        self.mm(V(bkO, bkO[p0:p0 + ln, h * 128:(h + 1) * 128]), V(QT, QT[:, h, cc0:cc0 + ln]), V(s16), False, True)
                        bkS = self.bank()
                        self.mm(V(bkS, bkS[:, 0:128]), V(KH, KH[p0:p0 + ln, bi, h * 128:(h + 1) * 128]), V(VR, VR[p0:p0 + ln, bi, h * 128:(h + 1) * 128]))
                        self.stt("dve", V(ST, ST[:, h, :], h, h + 1), V(ST, ST[:, h, :], h, h + 1), V(self.CDEC, self.CDEC[:, h:h + 1]), V(bkS, bkS[:, 0:128]), ALU.mult, ALU.add)
                self.cp("act", V(OS, OS[0:nr, :]), V(bkO, bkO[0:nr, :]))
                self.red("dve", V(st, st[0:nr, 0:4]), V(OS, OS[0:nr, :].rearrange("p (h e) -> p h e", h=4)), ALU.add)
                self.ts("dve", V(st, st[0:nr, 4:8]), V(st, st[0:nr, 0:4]), -1.0 / 128, None, ALU.mult)
                self.memset("dve", V(st, st[0:nr, 8:12]), 0.0)
                for h in range(4):
                    tm = self.tmp()
                    self.act(V(tm, tm[0:nr, 0:128]), V(OS, OS[0:nr, h * 128:(h + 1) * 128]), AF.Square, bias=V(st, st[0:nr, 4 + h:5 + h]), scale=1.0,
                             accum=V(st, st[0:nr, 8 + h:9 + h]))
                self.act(V(st, st[0:nr, 12:16]), V(st, st[0:nr, 8:12]), AF.Sqrt, bias=LN_EPS, scale=1.0 / 128)
                self.recip(V(st, st[0:nr, 16:20]), V(st, st[0:nr, 12:16]))
                for h in range(4):
                    self.ts("dve", V(OS, OS[0:nr, h * 128:(h + 1) * 128]), V(OS, OS[0:nr, h * 128:(h + 1) * 128]), V(st, st[0:nr, 4 + h:5 + h]),
                            V(st, st[0:nr, 16 + h:17 + h]), ALU.add, ALU.mult)
                self.tt("dve", V(OS, OS[0:nr, :]), V(OS, OS[0:nr, :]), V(self.GN, self.GN[0:nr, l, :]), ALU.mult)
                self.tt("dve", V(YTOK, YTOK[0:nr, bi, :]), V(OS, OS[0:nr, :]), V(SG, SG[0:nr, bi, :]), ALU.mult)
            self.transpose_to_fm(YTOK, YBT)
            if sample:
                for s in range(NS):
                    self.dma("sp", V(self.o_ret_s, self.o_ret_s[l, s].rearrange("h d e -> d h e")), V(SS[s]))
            elif t == cfg.NT - 1:
                self.dma("sp", V(self.o_ret_p, self.o_ret_p[l].rearrange("h d e -> d h e")), V(self.SL[l]))
            self.prefetch([(win, l, 0, 16, 9 * 512), (win, l, 0, 16, 10 * 512), (win, l, 0, 16, 11 * 512)])
        with P.scope():
            sb = P.sb
            KF0 = sb("KF0", [128, 512], F32)
            KF1 = sb("KF1", [128, 512], F32)
            KF = [KF0, KF1]
            s9 = self.wload(win, l, 0, 16, 9 * 512)
            for h in range(4):
                bk = self.bank()
                self.proj_fm(s9, h, V(bk, bk[:, 0:Tt]))
                self.ts("dve", V(self.DQ, self.DQ[:, h, 0:Tt]), V(bk, bk[:, 0:Tt]), 0.125, None, ALU.mult)
            for (u, dst_p, dst_s) in ((10, self.o_dk_p, self.o_dk_s), (11, self.o_dv_p, self.o_dv_s)):
                su = self.wload(win, l, 0, 16, u * 512)
                for bi, (c0, nr) in enumerate(self.tbs):
                    bk = self.bank()
                    self.proj_tm(su, c0, nr, bk)
                    kf = KF[(u + bi) % 2]
                    self.cp("act", V(kf, kf[0:nr, :]), V(bk, bk[0:nr, :]))
                    if sample:
                        for s in range(NS):
                            self.dma("sp", V(dst_s, dst_s[l, s]), V(kf, kf[32 * s:32 * s + 16, :]))
                    else:
                        r0 = t * T + c0
                        self.dma("sp", V(dst_p, dst_p[l, r0:r0 + nr, :], r0, r0 + nr), V(kf, kf[0:nr, :]))
            s12 = self.wload(win, l, 0, 16, 12 * 512)
            for h in range(4):
                bk = self.bank()
                self.proj_fm(s12, h, V(bk, bk[:, 0:Tt]))
                self.cp("dve", V(self.MQ, self.MQ[:, h, 0:Tt]), V(bk, bk[:, 0:Tt]))
            self.prefetch([(self.w_gt, l * 4, 0, 16, 0), (self.w_br, l * 4, 0, 4, 0), (self.w_gt, l * 4 + 1, 0, 16, 0)])
        import os as _os
        STG = int(_os.environ.get("K_STAGE", "9"))
        if STG >= 4:
            self.mem_attn(l)
        if STG >= 5:
            self.diff_attn(l)
            if self.dbg is not None and self.sample and l == 0:
                self.dma("sp", V(self.dbg, self.dbg.t.rearrange("p (h n) -> p h n", h=4)), V(self.YT[2], self.YT[2][:, :, 0:64]))
                self.P.flush()

    def mem_attn(self, l):
        P, cfg = self.P, self.cfg
        NS = cfg.NS
        sample = self.sample
        ntb = len(self.tbs)
        st = self.stat
        SC = 128 ** -0.5
        with P.scope():
            sb = P.sb
            PM = sb("PM", [128, 4, 256], BF16)
            PMT = sb("PMT", [128, 8, 128], BF16)
            YTOK = sb("YTOKd", [128, ntb, 512], BF16)
            if sample:
                MK = sb("MKs", [128, 2, 512], BF16)
                MKTs = sb("MKTs", [128, 4, 256], BF16)
                MVs = sb("MVs", [128, 2, 512], BF16)
                self.memset("dve", V(self.YT[3]), 0.0)
                qblocks = [(0, 32 * s, 16, s) for s in range(NS)]
            else:
                qblocks = [(bi, c0, nr, None) for bi, (c0, nr) in enumerate(self.tbs)]
            for (bi, c0, nq, s) in qblocks:
                if sample:
                    self.dma("pool", V(MK), V(self.c_mk, self.c_mk[l, s].rearrange("(b p) n -> p b n", p=128)))
                    self.dma("pool", V(MVs), V(self.c_mv, self.c_mv[l, s].rearrange("(b p) n -> p b n", p=128)))
                    for hp in range(2):
                        bk = self.bank()
                        for hh in range(2):
                            for kb in range(2):
                                self.mm(V(bk, bk[:, hh * 256 + kb * 128:hh * 256 + (kb + 1) * 128]), V(MK, MK[:, kb, (hp * 2 + hh) * 128:(hp * 2 + hh + 1) * 128]), V(self.identb))
                        self.cp("act", V(MKTs, MKTs[:, hp * 2:hp * 2 + 2, :]), V(bk, bk[:].rearrange("p (a n) -> p a n", a=2)))
                    mkt, mv = MKTs, MVs
                    prow = 32 * s
                else:
                    mkt, mv = self.MKT[l], self.MEMV[l]
                    prow = 0
                self.memset("dve", V(st, st[0:nq, 28:32]), 0.0)
                for hp in range(2):
                    bk = self.bank()
                    for hh in range(2):
                        h = hp * 2 + hh
                        self.mm(V(bk, bk[0:nq, hh * 256:(hh + 1) * 256]), V(self.MQ, self.MQ[:, h, c0:c0 + nq]), V(mkt, mkt[:, h, :]))
                    self.red("dve", V(st, st[0:nq, 20 + hp * 2:22 + hp * 2]), V(bk, bk[0:nq, :].rearrange("p (a n) -> p a n", a=2)), ALU.max)
                    self.ts("dve", V(st, st[0:nq, 24 + hp * 2:26 + hp * 2]), V(st, st[0:nq, 20 + hp * 2:22 + hp * 2]), -SC, None, ALU.mult)
                    for hh in range(2):
                        h = hp * 2 + hh
                        self.act(V(PM, PM[0:nq, h, :]), V(bk, bk[0:nq, hh * 256:(hh + 1) * 256]), AF.Exp, bias=V(st, st[0:nq, 24 + h:25 + h]), scale=SC,
                                 accum=V(st, st[0:nq, 28 + h:29 + h]))
                self.recip(V(st, st[0:nq, 32:36]), V(st, st[0:nq, 28:32]))
                for hp in range(2):
                    bk = self.bank()
                    for hh in range(2):
                        for kb in range(2):
                            h = hp * 2 + hh
                            self.mm(V(bk, bk[:, (hh * 2 + kb) * 128:(hh * 2 + kb) * 128 + nq]), V(PM, PM[0:nq, h, kb * 128:(kb + 1) * 128]), V(self.identb, self.identb[0:nq, 0:nq]))
                    self.cp("act", V(PMT, PMT[:, hp * 4:hp * 4 + 4, 0:nq]), V(bk, bk[:].rearrange("p (a n) -> p a n", a=4)[:, :, 0:nq]))
                bkO = self.bankO()
                for h in range(4):
                    for kb in range(2):
                        self.mm(V(bkO, bkO[0:nq, h * 128:(h + 1) * 128]), V(PMT, PMT[:, h * 2 + kb, 0:nq]), V(mv, mv[:, kb, h * 128:(h + 1) * 128]), kb == 0, kb == 1)
                for h in range(4):
                    self.ts("dve", V(YTOK, YTOK[0:nq, 0, h * 128:(h + 1) * 128]), V(bkO, bkO[0:nq, h * 128:(h + 1) * 128]), V(st, st[0:nq, 32 + h:33 + h]), None, ALU.mult)
                bk = self.bank()
                for h in range(4):
                    self.mm(V(bk, bk[:, h * 128:h * 128 + nq]), V(YTOK, YTOK[0:nq, 0, h * 128:(h + 1) * 128]), V(self.identb, self.identb[0:nq, 0:nq]))
                self.cp("act", V(self.YT[3], self.YT[3][:, :, c0:c0 + nq]), V(bk, bk[:].rearrange("p (h n) -> p h n", h=4)[:, :, 0:nq]))

    def diff_attn(self, l):
        P, cfg = self.P, self.cfg
        T, NS, PAST, SEQ = cfg.T, cfg.NS, cfg.PAST, cfg.SEQ
        t = self.tix
        sample = self.sample
        ntb = len(self.tbs)
        st = self.stat
        lam_init = 0.8 - 0.6 * math.exp(-0.3 * l)
        with P.scope():
            sb = P.sb
            if sample:
                nkb_max = PAST // 128 + 1
            else:
                nkb_max = (t + 1) * T // 128
            NK = nkb_max * 128
            KTOK = [sb(f"KTOK{i}", [128, 4, 128], BF16) for i in range(2)]
            KTh = sb("KTh", [128, NK], BF16)
            VH = sb("VH", [128, nkb_max, 128], BF16)
            P1 = sb("P1", [128, NK], BF16)
            P2 = sb("P2", [128, NK], BF16)
            AT = sb("AT", [128, nkb_max, 128], BF16)
            YTOK = sb("YTOKc", [128, 1, 128], BF16)
            NB = sb("NB", [128, 128], F32)
            MXS = sb("MXS", [128, 2, 16], F32)
            SMS = sb("SMS", [128, 2, 16], F32)
            OSc = sb("OSc", [128, 128], F32)
            if sample:
                self.memset("dve", V(self.YT[2]), 0.0)
            kt_i = 0
            seqs = list(range(NS)) if sample else [None]
            for sq in seqs:
                for h in range(4):
                    if sample:
                        blocks = [(self.c_k, self.c_k[l, sq, kb * 128:(kb + 1) * 128, h * 128:(h + 1) * 128], 128, self.c_v,
                                   self.c_v[l, sq, kb * 128:(kb + 1) * 128, h * 128:(h + 1) * 128]) for kb in range(PAST // 128)]
                        blocks.append((self.o_dk_s, self.o_dk_s[l, sq, :, h * 128:(h + 1) * 128], 16, self.o_dv_s, self.o_dv_s[l, sq, :, h * 128:(h + 1) * 128]))
                    else:
                        blocks = [(self.o_dk_p, self.o_dk_p[l, kb * 128:(kb + 1) * 128, h * 128:(h + 1) * 128], 128, self.o_dv_p,
                                   self.o_dv_p[l, kb * 128:(kb + 1) * 128, h * 128:(h + 1) * 128]) for kb in range(nkb_max)]
                    nkb = len(blocks)
                    for g0 in range(0, nkb, 4):
                        grp = blocks[g0:g0 + 4]
                        kt = KTOK[kt_i % 2]; kt_i += 1
                        bk = self.bank()
                        for gi, (kd, kap, nr, vd, vap) in enumerate(grp):
                            kb = g0 + gi
                            rr = (0, BIGR) if sample else (kb * 128, kb * 128 + nr)
                            self.dma("pool", V(kt, kt[0:nr, gi, :]), V(kd, kap, rr[0], rr[1]))
                            self.dma("pool", V(VH, VH[0:nr, kb, :], kb, kb + 1), V(vd, vap, rr[0], rr[1]))
                            self.mm(V(bk, bk[:, gi * 128:gi * 128 + nr]), V(kt, kt[0:nr, gi, :]), V(self.identb, self.identb[0:nr, 0:nr]))
                        ncols = sum(b[2] for b in grp)
                        if all(b[2] == 128 for b in grp):
                            self.cp("act", V(KTh, KTh[:, g0 * 128:g0 * 128 + ncols]), V(bk, bk[:, 0:ncols]))
                        else:
                            for gi, b in enumerate(grp):
                                self.cp("act", V(KTh, KTh[:, (g0 + gi) * 128:(g0 + gi) * 128 + b[2]]), V(bk, bk[:, gi * 128:gi * 128 + b[2]]))
                    if sample:
                        qbs = [(0, 32 * sq, 16, None, 32 * sq)]
                    else:
                        qbs = [(bi, c0, nr, (t * T + c0) // 128, 0) for bi, (c0, nr) in enumerate(self.tbs)]
                    for (bi, c0, nq, gb, prow) in qbs:
                        if sample:
                            nfar = PAST // 128 - 1
                            pieces = [(c, min(512, nfar * 128 - c), None) for c in range(0, nfar * 128, 512)]
                            pieces.append((nfar * 128, 128, 1))
                            pieces.append((PAST, 16, 0))
                            nk = PAST + 16
                        else:
                            nfar = max(gb - 1, 0)
                            pieces = [(c, min(512, nfar * 128 - c), None) for c in range(0, nfar * 128, 512)]
                            if gb >= 1:
                                pieces.append(((gb - 1) * 128, 128, 1))
                            pieces.append((gb * 128, 128, 0))
                            nk = (gb + 1) * 128
                        npc = len(pieces)
                        assert npc <= 16
                        for c in range(2):
                            for pi, (k0, kn, kind) in enumerate(pieces):
                                bk = self.bank()
                                self.mm(V(bk, bk[0:nq, 0:kn]), V(self.DQ, self.DQ[c * 64:(c + 1) * 64, h, c0:c0 + nq]), V(KTh, KTh[c * 64:(c + 1) * 64, k0:k0 + kn]))
                                self.red("dve", V(MXS, MXS[0:nq, c, pi:pi + 1]), V(bk, bk[0:nq, 0:kn]), ALU.max)
                            self.red("dve", V(st, st[0:nq, 40 + c:41 + c]), V(MXS, MXS[0:nq, c, 0:npc]), ALU.max)
                            self.ts("dve", V(st, st[0:nq, 42 + c:43 + c]), V(st, st[0:nq, 40 + c:41 + c]), V(self.CF, self.CF[0:nq, 4 + h:5 + h]), -1.0, ALU.add, ALU.mult)
                            self.tt("dve", V(st, st[0:nq, 44 + c:45 + c]), V(st, st[0:nq, 42 + c:43 + c]), V(self.CF, self.CF[0:nq, h:h + 1]), ALU.add)
                        for c in range(2):
                            PP = P1 if c == 0 else P2
                            self.memset("dve", V(SMS, SMS[0:nq, c, :]), 0.0)
                            for pi, (k0, kn, kind) in enumerate(pieces):
                                bk = self.bank()
                                self.mm(V(bk, bk[0:nq, 0:kn]), V(self.DQ, self.DQ[c * 64:(c + 1) * 64, h, c0:c0 + nq]), V(KTh, KTh[c * 64:(c + 1) * 64, k0:k0 + kn]))
                                if kind is None:
                                    self.act(V(PP, PP[0:nq, k0:k0 + kn]), V(bk, bk[0:nq, 0:kn]), AF.Exp, bias=V(st, st[0:nq, 44 + c:45 + c]), scale=1.0,
                                             accum=V(SMS, SMS[0:nq, c, pi:pi + 1]))
                                else:
                                    self.tt("dve", V(NB, NB[0:nq, 0:kn]), V(bk, bk[0:nq, 0:kn]), V(self.BT, self.BT[0:nq, kind, h, 0:kn]), ALU.add)
                                    self.act(V(PP, PP[0:nq, k0:k0 + kn]), V(NB, NB[0:nq, 0:kn]), AF.Exp, bias=V(st, st[0:nq, 42 + c:43 + c]), scale=1.0,
                                             accum=V(SMS, SMS[0:nq, c, pi:pi + 1]))
                            self.red("dve", V(st, st[0:nq, 46 + c:47 + c]), V(SMS, SMS[0:nq, c, 0:npc]), ALU.add)
                        self.recip(V(st, st[0:nq, 48:50]), V(st, st[0:nq, 46:48]))
                        self.tt("dve", V(st, st[0:nq, 50:51]), V(st, st[0:nq, 49:50]), V(self.LAMC, self.LAMC[0:nq, l, 1:2]), ALU.mult)
                        self.ts("dve", V(P1, P1[0:nq, 0:nk]), V(P1, P1[0:nq, 0:nk]), V(st, st[0:nq, 48:49]), None, ALU.mult)
                        self.stt("dve", V(P1, P1[0:nq, 0:nk]), V(P2, P2[0:nq, 0:nk]), V(st, st[0:nq, 50:51]), V(P1, P1[0:nq, 0:nk]), ALU.mult, ALU.add)
                        kbl = [(kb, 128) for kb in range(nk // 128)]
                        if nk % 128:
                            kbl.append((nk // 128, nk % 128))
                        for g0 in range(0, len(kbl), 4):
                            grp = kbl[g0:g0 + 4]
                            bk = self.bank()
                            for gi, (kb, nr) in enumerate(grp):
                                self.mm(V(bk, bk[0:nr, gi * 128:gi * 128 + nq]), V(P1, P1[0:nq, kb * 128:kb * 128 + nr]), V(self.identb, self.identb[0:nq, 0:nq]))
                            if all(b[1] == 128 for b in grp):
                                self.cp("act", V(AT, AT[:, g0:g0 + len(grp), 0:nq], g0, g0 + len(grp)),
                                        V(bk, bk[:, 0:len(grp) * 128].rearrange("p (a n) -> p a n", n=128)[:, :, 0:nq]))
                            else:
                                for gi, (kb, nr) in enumerate(grp):
                                    self.cp("act", V(AT, AT[0:nr, kb, 0:nq], kb, kb + 1), V(bk, bk[0:nr, gi * 128:gi * 128 + nq]))
                        bkO = self.bankO()
                        for i, (kb, nr) in enumerate(kbl):
                            self.mm(V(bkO, bkO[0:nq, 0:128]), V(AT, AT[0:nr, kb, 0:nq], kb, kb + 1), V(VH, VH[0:nr, kb, :], kb, kb + 1), i == 0, i == len(kbl) - 1)
                        self.cp("act", V(OSc, OSc[0:nq, :]), V(bkO, bkO[0:nq, 0:128]))
                        tm = self.tmp()
                        self.memset("dve", V(st, st[0:nq, 52:53]), 0.0)
                        self.act(V(tm, tm[0:nq, 0:128]), V(OSc, OSc[0:nq, :]), AF.Square, accum=V(st, st[0:nq, 52:53]))
                        self.act(V(st, st[0:nq, 53:54]), V(st, st[0:nq, 52:53]), AF.Sqrt, bias=LN_EPS, scale=1.0 / 128)
                        self.recip(V(st, st[0:nq, 54:55]), V(st, st[0:nq, 53:54]))
                        self.ts("dve", V(OSc, OSc[0:nq, :]), V(OSc, OSc[0:nq, :]), V(st, st[0:nq, 54:55]), 1.0 - lam_init, ALU.mult, ALU.mult)
                        self.tt("dve", V(YTOK, YTOK[0:nq, 0, 0:128]), V(OSc, OSc[0:nq, :]), V(self.SUBG, self.SUBG[0:nq, l, :]), ALU.mult)
                        bk = self.bank()
                        self.mm(V(bk, bk[:, 0:nq]), V(YTOK, YTOK[0:nq, 0, 0:128]), V(self.identb, self.identb[0:nq, 0:nq]))
                        self.cp("act", V(self.YT[2], self.YT[2][:, h, c0:c0 + nq]), V(bk, bk[:, 0:nq]))

    def gates(self, l, nxt=None):
        P = self.P
        Tt = self.tile_T
        X, X16 = self.X, self.X16
        with P.scope():
            ACC = P.sb("ACC", [128, 4, self.cfg.T], F32)
            M16 = P.sb("M16", [128, DC, self.cfg.T], BF16)
            SQ = P.sb("SQg", [128, DC, self.cfg.T], BF16)
            for g in range(4):
                for b in range(4):
                    sg_ = self.wload(self.w_gt, l * 4 + b, 0, 16, g * 512)
                    sb_ = self.wload(self.w_br, l * 4 + b, 0, 4, g * 512)
                    for c in range(4):
                        cg = g * 4 + c
                        bk = self.bank()
                        bkp = self.bank()
                        for k in range(DC):
                            self.mm(V(bk, bk[:, 0:Tt]), self.W(sg_, k, c * 128, (c + 1) * 128), V(X16, X16[:, k, 0:Tt], k, k + 1), k == 0, k == DC - 1)
                        for k in range(4):
                            self.mm(V(bkp, bkp[:, 0:Tt]), self.W(sb_, k, c * 128, (c + 1) * 128), V(self.YT[b], self.YT[b][:, k, 0:Tt]), k == 0, k == 3)
                        tm = self.tmp()
                        sgv = V(tm, tm[:, 0:Tt])
                        bi = (l * 4 + b) * 16 + cg
                        self.act(sgv, V(bk, bk[:, 0:Tt]), AF.Sigmoid, bias=V(self.bg, self.bg[:, bi:bi + 1]), scale=1.0)
                        acc = V(ACC, ACC[:, c, 0:Tt], c, c + 1)
                        if b == 0:
                            self.tt("dve", acc, sgv, V(bkp, bkp[:, 0:Tt]), ALU.mult)
                        else:
                            self.tt("dve", sgv, sgv, V(bkp, bkp[:, 0:Tt]), ALU.mult)
                            if b < 3:
                                self.tt("dve", acc, acc, sgv, ALU.add)
                            else:
                                self.tt("dve", V(M16, M16[:, cg, 0:Tt], cg, cg + 1), acc, sgv, ALU.add)
            for g in range(4):
                s = self.wload(self.w_o, l, 0, 16, g * 512)
                for c in range(4):
                    bk = self.bank()
                    for k in range(DC):
                        self.mm(V(bk, bk[:, 0:Tt]), self.W(s, k, c * 128, (c + 1) * 128), V(M16, M16[:, k, 0:Tt], k, k + 1), k == 0, k == DC - 1)
                    cg = g * 4 + c
                    xc = V(X, X[:, cg, 0:Tt], cg, cg + 1)
                    self.stt("dve", xc, xc, self.cfg.ALPHA, V(bk, bk[:, 0:Tt]), ALU.mult, ALU.add)
            self.prefetch(nxt)
            self.layernorm(l, 1, SQ)


def _t5_bucket(rel):
    import jax
    import jax.numpy as jnp
    cpu = jax.devices("cpu")[0]
    with jax.default_device(cpu):
        rel = jnp.asarray(rel, dtype=jnp.int32)
        nb = 16
        max_exact = 8
        n = jnp.abs(rel)
        nf = jnp.maximum(n, 1).astype(jnp.float32)
        large = max_exact + (jnp.log(nf / max_exact) / math.log(128 / max_exact) * (nb - max_exact)).astype(jnp.int32)
        large = jnp.minimum(large, nb - 1)
        out = jnp.where(rel > 0, nb, 0) + jnp.where(n < max_exact, n, large)
        return np.asarray(out)


def make_consts(cfg):
    SEQ, PAST, NS, L, T, NT = cfg.SEQ, cfg.PAST, cfg.NS, cfg.DEPTH, cfg.T, cfg.NT
    f32 = np.float32
    c = {}
    inv = (f32(10000.0) ** (-(np.arange(64, dtype=f32) / f32(64)))).astype(f32)
    rope = np.zeros((NT + 1, 2, 128, T), f32)
    dimi = np.arange(128) % 64
    sign = np.where(np.arange(128) < 64, -1.0, 1.0)

    def fill(ti, cols, pos):
        ang = (pos.astype(f32)[None, :] * inv[dimi][:, None]).astype(f32).astype(np.float64)
        rope[ti, 0][:, cols] = np.cos(ang)
        rope[ti, 1][:, cols] = np.sin(ang) * sign[:, None]
    for t in range(NT):
        fill(t, np.arange(T), np.arange(t * T, (t + 1) * T))
    for s in range(NS):
        fill(NT, 32 * s + np.arange(16), PAST + np.arange(16))
    c["rope"] = rope
    gam = np.array([1.0 - 2.0 ** (-5 - h) for h in range(4)], np.float64)
    sc = 128.0 ** -0.5
    qd = np.zeros((2, 128, 4, T), np.float64)
    for h in range(4):
        qd[0, :, h, :] = gam[h] ** ((np.arange(T) % 64) + 1)[None, :]
        for s in range(NS):
            qd[1, :, h, 32 * s:32 * s + 16] = gam[h] ** (np.arange(16) + 1)[None, :]
    c["qd"] = qd.reshape(2, 128, 4 * T).astype(f32).astype(BF)
    rm = np.zeros((2, 128, 4, 128), np.float64)
    kdx = np.zeros((2, 128, 4, 128), np.float64)
    j = np.arange(128)[:, None]
    i = np.arange(128)[None, :]
    for h in range(4):
        ok = (j // 64 == i // 64) & ((j % 64) <= (i % 64))
        rm[0, :, h, :] = np.where(ok, sc * gam[h] ** (-((j % 64) + 1.0)), 0.0)
        kdx[0, :, h, :] = (sc * gam[h] ** (63.0 - (np.arange(128) % 64)))[:, None]
        for s in range(NS):
            jl = np.arange(16)[:, None]
            il = np.arange(16)[None, :]
            rm[1, 32 * s:32 * s + 16, h, 32 * s:32 * s + 16] = np.where(jl <= il, sc * gam[h] ** (-(jl + 1.0)), 0.0)
            kdx[1, 32 * s:32 * s + 16, h, :] = (sc * gam[h] ** (15.0 - np.arange(16)))[:, None]
    c["rm"] = rm.reshape(2, 128, 512).astype(f32)
    c["kdx"] = kdx.reshape(2, 128, 512).astype(f32)
    cd = np.zeros((2, 128, 4), np.float64)
    cd[0] = (gam ** 64)[None, :]
    cd[1] = (gam ** 16)[None, :]
    c["cdec"] = cd.astype(f32)
    q = np.arange(128)[None, :]
    k = np.arange(128)[:, None]
    oh = np.zeros((32, 2, 128, 128), f32)
    for ti in range(2):
        bk = _t5_bucket(k - q - 128 * ti)
        for b in range(32):
            oh[b, ti] = (bk == b)
    c["oh"] = oh.reshape(32, 2 * 128 * 128)
    qq = np.arange(128)[:, None]
    kk = np.arange(128)[None, :]
    c["mask0"] = np.where((kk // 64) > (qq // 64), NEG, 0.0).astype(f32)
    c["ident"] = np.eye(128, dtype=f32)
    c["identb"] = np.eye(128, dtype=f32).astype(BF)
    c["ones"] = np.full((128, 128), 1.0 / D, f32).astype(BF)
    return c


def prep_shared(inp, cfg):
    L = cfg.DEPTH
    f = lambda a: np.ascontiguousarray(np.asarray(a, dtype=np.float32))
    sh = {}
    def relay_up(w):
        w = f(w)
        a = w[:, :, :DFF].reshape(w.shape[0], D, 22, 256)
        b = w[:, :, DFF:].reshape(w.shape[0], D, 22, 256)
        return np.ascontiguousarray(np.concatenate([a, b], axis=3).reshape(w.shape[0], D, 2 * DFF))
    sh["w_up1"] = relay_up(inp["ffn1_w_up"]); sh["w_dn1"] = f(inp["ffn1_w_down"])
    sh["w_up2"] = relay_up(inp["ffn2_w_up"]); sh["w_dn2"] = f(inp["ffn2_w_down"])
    w_in = f(inp["w_in"])
    sl = lambda i: w_in[:, :, i * 512:(i + 1) * 512]
    swp = np.concatenate([np.arange(64, 128), np.arange(0, 64)])
    perm = np.concatenate([h * 128 + swp for h in range(4)])
    sh["w_in"] = np.ascontiguousarray(np.concatenate(
        [sl(0), sl(1), sl(2), sl(3), sl(3)[:, :, perm], sl(4), sl(4)[:, :, perm], sl(5), sl(6), sl(7), sl(8), sl(9), sl(10)], axis=2))
    sh["w_mkv"] = f(inp["w_mem_kv"])
    sh["w_br"] = f(inp["w_branch"]).reshape(L * 4, 512, D)
    sh["w_gt"] = f(inp["w_gate"]).reshape(L * 4, D, D)
    sh["w_o"] = f(inp["w_o"])

    def pc(a):
        a = f(a)
        lead = a.shape[:-1]
        a = a.reshape(lead + (16, 128))
        a = np.moveaxis(a, -1, 0)
        return np.ascontiguousarray(a).reshape(128, -1)
    lnp = np.stack([f(inp[k]) for k in ("ln1_g", "ln1_b", "ln2_g", "ln2_b", "ln3_g", "ln3_b")], axis=1)
    sh["lnp"] = pc(lnp)
    sh["bgate"] = pc(inp["b_gate"])
    cw = f(inp["conv_w"]).reshape(L, 3, 4, 128)
    sh["convw"] = np.ascontiguousarray(np.moveaxis(cw, -1, 0)).reshape(128, -1)
    sh["gn"] = f(inp["ret_gn_g"]); sh["subg"] = f(inp["diff_subln_g"])
    sh["lam"] = f(inp["diff_lambda"]).reshape(L, 256)
    sh["relb"] = f(inp["rel_bias"]).reshape(1, 128)
    sh["relb2"] = f(inp["rel_bias"])
    sh.update(make_consts(cfg))
    return sh


def prep_core(inp, cfg, core, nb):
    f = lambda a: np.ascontiguousarray(np.asarray(a, dtype=np.float32))
    NS, L = cfg.NS, cfg.DEPTH
    b = core % nb
    s0 = core * NS
    m = {}
    m["xp"] = f(inp["x_prompt"][b])
    m["xs"] = f(inp["x_sample"][s0:s0 + NS]).reshape(NS * 16, D)
    m["memp"] = f(inp["mem_prompt"][b])
    m["st_conv"] = f(inp["state_conv"][:, s0:s0 + NS])
    m["st_ret"] = f(inp["state_ret"][:, s0:s0 + NS])
    m["c_k"] = f(inp["cache_diff_k"][:, s0:s0 + NS]).reshape(L, NS, cfg.PAST, 512)
    m["c_v"] = f(inp["cache_diff_v"][:, s0:s0 + NS]).reshape(L, NS, cfg.PAST, 512)
    m["c_mk"] = f(inp["cache_mem_k"][:, s0:s0 + NS]).reshape(L, NS, 256, 512)
    m["c_mv"] = f(inp["cache_mem_v"][:, s0:s0 + NS]).reshape(L, NS, 256, 512)
    return m


def assemble(results, cfg, nb, ncores):
    L, NS, SEQ = cfg.DEPTH, cfg.NS, cfg.SEQ
    R = results
    pc = list(range(nb))
    f = lambda a: np.asarray(a, dtype=np.float32)
    yp = np.stack([f(R[c]["yp"]) for c in pc])
    ys = np.concatenate([f(R[c]["ys"]).reshape(NS, 16, D) for c in range(ncores)])
    conv_p = np.stack([f(R[c]["o_conv_p"]) for c in pc], axis=1)
    ret_p = np.stack([f(R[c]["o_ret_p"]) for c in pc], axis=1)
    dk_p = np.stack([f(R[c]["o_dk_p"]).reshape(L, SEQ, 4, 128) for c in pc], axis=1)
    dv_p = np.stack([f(R[c]["o_dv_p"]).reshape(L, SEQ, 4, 128) for c in pc], axis=1)
    mk_p = np.stack([f(R[c]["o_mk_p"]).reshape(L, 256, 4, 128) for c in pc], axis=1)
    mv_p = np.stack([f(R[c]["o_mv_p"]).reshape(L, 256, 4, 128) for c in pc], axis=1)
    conv_s = np.concatenate([f(R[c]["o_conv_s"]) for c in range(ncores)], axis=1)
    ret_s = np.concatenate([f(R[c]["o_ret_s"]) for c in range(ncores)], axis=1)
    dk_s = np.concatenate([f(R[c]["o_dk_s"]).reshape(L, NS, 16, 4, 128) for c in range(ncores)], axis=1)
    dv_s = np.concatenate([f(R[c]["o_dv_s"]).reshape(L, NS, 16, 4, 128) for c in range(ncores)], axis=1)
    return (yp, ys, conv_p, ret_p, dk_p, dv_p, mk_p, mv_p, conv_s, ret_s, dk_s, dv_s)


def kernel(**inputs):
    xp = inputs["x_prompt"]
    nb, SEQ = xp.shape[0], xp.shape[1]
    ncores = 8
    NS = inputs["x_sample"].shape[0] // ncores
    cfg = Cfg(SEQ=SEQ, PAST=inputs["cache_diff_k"].shape[2], NS=NS, DEPTH=inputs["w_o"].shape[0], T=512)
    nc = KB(cfg).build()
    sh = prep_shared(inputs, cfg)
    in_maps = []
    for c in range(ncores):
        m = dict(sh)
        m.update(prep_core(inputs, cfg, c, nb))
        in_maps.append(m)
    res = run_bass_kernel_spmd(nc, in_maps, core_ids=list(range(ncores)))
    return assemble(res.results, cfg, nb, ncores)
```

```python
import contextlib
import numpy as np
import concourse.bass as bass
import concourse.mybir as mybir

F32 = mybir.dt.float32
BF16 = mybir.dt.bfloat16
AF = mybir.ActivationFunctionType
ALU = mybir.AluOpType
AX = mybir.AxisListType

COMPUTE = ("pe", "act", "dve", "pool")
EPOCH = 24000


class Tile:
    __slots__ = ("name", "t", "recs", "space")

    def __init__(self, name, t, space):
        self.name = name
        self.t = t
        self.space = space
        self.recs = []

    def __getitem__(self, idx):
        return self.t[idx]


class Op:
    __slots__ = ("eng", "idx", "fn", "waits", "needs_inc", "is_dma", "slot", "val", "ticket", "prewait")

    def __init__(self, eng, idx, fn, is_dma):
        self.eng = eng
        self.idx = idx
        self.fn = fn
        self.waits = []
        self.needs_inc = False
        self.is_dma = is_dma
        self.slot = None
        self.val = None
        self.ticket = None
        self.prewait = None


class Prog:
    def __init__(self, nc, n_dma_sems=None):
        self.nc = nc
        self.stack = contextlib.ExitStack()
        self.ops = {e: [] for e in ("pe", "act", "dve", "pool", "sp")}
        self.known = {e: {c: -1 for c in COMPUTE} for e in self.ops}
        self.known_dma = {e: set() for e in self.ops}
        self.ndma = {"sp": 0, "pool": 0, "act": 0}
        self.nslots = n_dma_sems or {"sp": 16, "pool": 8, "act": 4}
        self.dma_ops = {"sp": [], "pool": [], "act": []}
        self.tiles = []
        self.stacks = [self.stack]
        self.flushed = {e: 0 for e in self.ops}
        self.tk = {e: 0 for e in COMPUTE}
        self.lastt = {e: None for e in COMPUTE}
        import os as _os
        NEP = int(_os.environ.get('NEP', '8'))
        self.csem = {e: [self.stack.enter_context(nc.semaphore(f"s_{e}_{i}")) for i in range(NEP)] for e in COMPUTE}
        self.dsem = {q: [self.stack.enter_context(nc.semaphore(f"d_{q}_{i}")) for i in range(self.nslots[q])]
                     for q in ("sp", "pool")}
        self.nbank = 0
        self.banks = None

    def sb(self, name, shape, dtype):
        self.uid = getattr(self, "uid", 0) + 1
        t = self.stacks[-1].enter_context(self.nc.sbuf_tensor(f"sb{self.uid}_{name}", list(shape), dtype))
        tl = Tile(name, t, "sb")
        self.tiles.append(tl)
        return tl

    def ps(self, name, shape, dtype):
        t = self.stack.enter_context(self.nc.psum_tensor("ps_" + name, list(shape), dtype))
        tl = Tile(name, t, "ps")
        self.tiles.append(tl)
        return tl

    def dram(self, name, shape, dtype, kind):
        t = self.nc.dram_tensor(name, list(shape), dtype, kind=kind)
        tl = Tile(name, t.ap(), "dram")
        self.tiles.append(tl)
        return tl

    def _deps(self, reads, writes):
        deps = []
        for (tl, lo, hi) in reads:
            for r in tl.recs:
                if r[0] < hi and lo < r[1]:
                    if r[2] is not None:
                        deps.append((r[2], "raw"))
        for (tl, lo, hi) in writes:
            for r in tl.recs:
                if r[0] < hi and lo < r[1]:
                    if r[2] is not None:
                        deps.append((r[2], "waw"))
                    for o in r[3].values():
                        deps.append((o, "war"))
        return deps

    def _update(self, op, reads, writes):
        for (tl, lo, hi) in reads:
            covered = []
            for r in tl.recs:
                if r[0] < hi and lo < r[1]:
                    r[3][op.eng if not op.is_dma else ("dma", id(op))] = op
                    covered.append((r[0], r[1]))
            covered.sort()
            cur = lo
            newrecs = []
            for (a, b) in covered:
                if a > cur:
                    newrecs.append([cur, min(a, hi), None, {}])
                cur = max(cur, b)
            if cur < hi:
                newrecs.append([cur, hi, None, {}])
            for nr in newrecs:
                nr[3][op.eng if not op.is_dma else ("dma", id(op))] = op
                tl.recs.append(nr)
        for (tl, lo, hi) in writes:
            keep = []
            for r in tl.recs:
                if r[0] >= lo and r[1] <= hi:
                    continue
                if r[0] < hi and lo < r[1]:
                    if r[0] < lo:
                        keep.append([r[0], lo, r[2], dict(r[3])])
                    if r[1] > hi:
                        keep.append([hi, r[1], r[2], dict(r[3])])
                    continue
                keep.append(r)
            keep.append([lo, hi, op, {}])
            tl.recs = keep

    def add(self, eng, fn, reads=(), writes=(), dma=False):
        ops = self.ops[eng]
        op = Op(eng, len(ops), fn, dma)
        reads = [(t, 0, 1) if t.space == "ps" else (t, lo, hi) for (t, lo, hi) in reads]
        writes = [(t, 0, 1) if t.space == "ps" else (t, lo, hi) for (t, lo, hi) in writes]
        deps = self._deps(reads, writes)
        kn = self.known[eng]
        kd = self.known_dma[eng]
        for (d, kind) in deps:
            if d.is_dma:
                if id(d) in kd:
                    continue
                kd.add(id(d))
                op.waits.append(d)
            else:
                if d.eng == eng and not dma:
                    if eng == "pe":
                        continue
                if kn[d.eng] >= d.idx:
                    continue
                kn[d.eng] = d.idx
                d.needs_inc = True
                op.waits.append(d)
        if dma:
            k = self.ndma[eng]
            ns = self.nslots[eng]
            op.slot = k % ns
            op.val = 16 * (k // ns + 1)
            if k >= ns:
                prev = self.dma_ops[eng][k - ns]
                if id(prev) not in kd:
                    kd.add(id(prev))
                    op.waits.append(prev)
            self.ndma[eng] = k + 1
            self.dma_ops[eng].append(op)
        ops.append(op)
        self._update(op, reads, writes)
        return op

    @contextlib.contextmanager
    def scope(self):
        st = contextlib.ExitStack()
        self.stacks.append(st)
        try:
            yield
        finally:
            self.flush()
            self.stacks.pop()
            st.close()

    def flush(self, final=False):
        nc = self.nc
        csem, dsem = self.csem, self.dsem
        new = {e: self.ops[e][self.flushed[e]:] for e in self.ops}
        for e in COMPUTE:
            real = [op for op in new[e] if op.fn is not None and not op.is_dma]
            if real:
                real[-1].needs_inc = True
            for op in new[e]:
                if op.needs_inc and not op.is_dma and op.fn is not None:
                    op.ticket = self.tk[e]
                    self.tk[e] += 1
                    self.lastt[e] = op
            assert self.tk[e] < EPOCH * len(csem[e]), "too many tickets"

        def wait_for(engobj, d):
            if d.is_dma:
                engobj.wait_ge(dsem[d.eng][d.slot], d.val)
            else:
                engobj.wait_ge(csem[d.eng][d.ticket // EPOCH], d.ticket % EPOCH + 1)

        def run(ename):
            def body(engobj):
                for op in new[ename]:
                    for d in op.waits:
                        wait_for(engobj, d)
                    if op.fn is None:
                        continue
                    inst = op.fn(engobj)
                    if op.is_dma:
                        inst.then_inc(dsem[op.eng][op.slot], 16)
                    elif op.needs_inc:
                        inst.then_inc(csem[op.eng][op.ticket // EPOCH], 1)
                if final and ename == "sp":
                    for q, lst in self.dma_ops.items():
                        ns = self.nslots[q]
                        for op in lst[-ns:]:
                            engobj.wait_ge(dsem[q][op.slot], op.val)
                    for e in COMPUTE:
                        if self.lastt[e] is not None:
                            wait_for(engobj, self.lastt[e])
            return body

        if not hasattr(self, "pending"):
            self.pending = {e: [] for e in self.ops}
        for e in self.ops:
            self.pending[e].extend(new[e])
        if final:
            new = self.pending
            with nc.Block() as block:
                block.sync(run("sp"))
                block.tensor(run("pe"))
                block.scalar(run("act"))
                block.vector(run("dve"))
                block.gpsimd(run("pool"))
        for e in self.ops:
            self.flushed[e] = len(self.ops[e])
        if final:
            return
        for e in self.ops:
            op = Op(e, len(self.ops[e]), None, False)
            for c in COMPUTE:
                lt = self.lastt[c]
                if lt is not None and self.known[e][c] < lt.idx:
                    op.waits.append(lt)
                    self.known[e][c] = lt.idx
            for q, lst in self.dma_ops.items():
                for d in lst[-self.nslots[q]:]:
                    if id(d) not in self.known_dma[e]:
                        self.known_dma[e].add(id(d))
                        op.waits.append(d)
            self.ops[e].append(op)
        for tl in self.tiles:
            tl.recs = []

    def close(self):
        self.stack.close()


import math
import numpy as np
import ml_dtypes
import concourse.bass as bass
import concourse.mybir as mybir
from concourse.bass_utils import run_bass_kernel_spmd

D = 2048
DC = 16
DFF = 5632
FC = 44
BIGR = 1 << 30
NEG = -1e30
LN_EPS = 1e-5
BF = ml_dtypes.bfloat16


class Cfg:
    def __init__(self, SEQ=4096, PAST=2048, NS=2, DEPTH=2, T=512):
        self.SEQ, self.PAST, self.NS, self.DEPTH, self.T = SEQ, PAST, NS, DEPTH, T
        self.NT = SEQ // T
        self.ALPHA = (2 * DEPTH) ** 0.25


def V(t, ap=None, lo=0, hi=BIGR):
    return (t, t.t[:] if ap is None else ap, lo, hi)


def _r(v):
    return (v[0], v[2], v[3])


class KB:
    def __init__(self, cfg):
        self.cfg = cfg
        self.nc = bass.Bass("TRN2", target_bir_lowering=False)
        self.P = Prog(self.nc)
        self.bank_i = 0
        self.wslot_i = 0
        self.tmp_i = 0

    def mm(self, o, l, r, start=True, stop=True):
        self.P.add("pe", lambda e: e.matmul(o[1], lhsT=l[1], rhs=r[1], start=start, stop=stop),
                   reads=[_r(l), _r(r)], writes=[_r(o)])

    def act(self, o, i, func, bias=None, scale=None, accum=None):
        if i[0].space == "ps":
            shp = list(i[1].shape)
            assert len(shp) == 2, shp
            sc = self.tmp()
            sv = V(sc, sc[0:shp[0], 0:shp[1]])
            self.cp("dve", sv, i)
            i = sv
        reads = [_r(i)]
        kw = {}
        if bias is not None:
            if isinstance(bias, tuple):
                reads.append(_r(bias)); kw["bias"] = bias[1]
            else:
                kw["bias"] = float(bias)
        if scale is not None:
            if isinstance(scale, tuple):
                reads.append(_r(scale)); kw["scale"] = scale[1]
            else:
                kw["scale"] = float(scale)
        writes = [_r(o)]
        if accum is not None:
            kw["accum_out"] = accum[1]; writes.append(_r(accum))
        self.P.add("act", lambda e: e.activation(out=o[1], in_=i[1], func=func, **kw), reads=reads, writes=writes)

    def tt(self, eng, o, a, b, op):
        self.P.add(eng, lambda e: e.tensor_tensor(out=o[1], in0=a[1], in1=b[1], op=op), reads=[_r(a), _r(b)], writes=[_r(o)])

    def ts(self, eng, o, a, s1, s2=None, op0=ALU.mult, op1=None):
        reads = [_r(a)]
        a1 = s1
        if isinstance(s1, tuple):
            reads.append(_r(s1)); a1 = s1[1]
        a2 = s2
        if isinstance(s2, tuple):
            reads.append(_r(s2)); a2 = s2[1]
        kw = {} if op1 is None else {"op1": op1}
        self.P.add(eng, lambda e: e.tensor_scalar(out=o[1], in0=a[1], scalar1=a1, scalar2=a2, op0=op0, **kw), reads=reads, writes=[_r(o)])

    def stt(self, eng, o, a, s, b, op0, op1):
        reads = [_r(a), _r(b)]
        sc = s
        if isinstance(s, tuple):
            reads.append(_r(s)); sc = s[1]
        self.P.add(eng, lambda e: e.scalar_tensor_tensor(out=o[1], in0=a[1], scalar=sc, in1=b[1], op0=op0, op1=op1), reads=reads, writes=[_r(o)])

    def cp(self, eng, o, a):
        if eng == "act" and a[0].space == "ps":
            eng = "dve"
        if eng == "act":
            return self.act(o, a, AF.Copy)
        self.P.add(eng, lambda e: e.tensor_copy(out=o[1], in_=a[1]), reads=[_r(a)], writes=[_r(o)])

    def red(self, eng, o, a, op):
        self.P.add(eng, lambda e: e.tensor_reduce(out=o[1], in_=a[1], axis=AX.X, op=op), reads=[_r(a)], writes=[_r(o)])

    def recip(self, o, a):
        self.P.add("dve", lambda e: e.reciprocal(out=o[1], in_=a[1]), reads=[_r(a)], writes=[_r(o)])

    def memset(self, eng, o, val):
        self.P.add(eng, lambda e: e.memset(o[1], val), writes=[_r(o)])

    def dma(self, q, o, i, slow=False):
        kw = {"allow_slow_non_contiguous": True} if slow else {}
        self.P.add(q, lambda e: e.dma_start(out=o[1], in_=i[1], **kw), reads=[_r(i)], writes=[_r(o)], dma=True)

    def bank(self):
        b = self.banks[self.bank_i % 6]
        self.bank_i += 1
        return b

    def bankO(self):
        self.banko_i = getattr(self, "banko_i", 0) + 1
        return self.banks[6 + self.banko_i % 2]

    def tmp(self):
        t = self.tmps[self.tmp_i % len(self.tmps)]
        self.tmp_i += 1
        return t

    def wload(self, wt, lidx, row0, kc, col0, ncols=512, prefetch=False, cache=True):
        pf = self.__dict__.setdefault("pf", {})
        key = (wt.name, lidx, row0, kc, col0, ncols)
        if not prefetch and key in pf:
            return pf.pop(key)
        s = self.wslot_i % self.NBUF
        self.wslot_i += 1
        for k_ in [k_ for k_, v_ in pf.items() if v_ == s]:
            del pf[k_]
        if prefetch:
            pf[key] = s
        uids = self.__dict__.setdefault("unit_ids", {})
        if cache and key in uids and self.tix > 0:
            u = uids[key]
            wscr = self.wscrs[u // 64]
            src = wscr.t[u % 64, :, 0:kc * ncols].rearrange("p (k n) -> p k n", k=kc)
            self.dma("sp", V(self.WR, self.WR[:, s, 0:kc, 0:ncols], s, s + 1), V(wscr, src, u % 64, u % 64 + 1))
            return s
        src = wt.t[lidx, row0:row0 + kc * 128, col0:col0 + ncols].rearrange("(k p) n -> p k n", p=128)
        self.dma("pool", V(self.WR, self.WR[:, s, 0:kc, 0:ncols], s, s + 1), V(wt, src))
        if cache and self.tix == 0 and key not in uids and len(uids) < self.NUNITS:
            u = uids[key] = len(uids)
            wscr = self.wscrs[u // 64]
            dst = wscr.t[u % 64, :, 0:kc * ncols].rearrange("p (k n) -> p k n", k=kc)
            self.dma("sp", V(wscr, dst, u % 64, u % 64 + 1), V(self.WR, self.WR[:, s, 0:kc, 0:ncols], s, s + 1))
        return s

    def prefetch(self, units):
        for u in (units or [])[:self.NBUF]:
            self.wload(*u, prefetch=True)

    def W(self, s, k, c0, c1):
        return V(self.WR, self.WR[:, s, k, c0:c1], s, s + 1)

    def build(self):
        cfg, P = self.cfg, self.P
        SEQ, PAST, NS, DEPTH, T, NT = cfg.SEQ, cfg.PAST, cfg.NS, cfg.DEPTH, cfg.T, cfg.NT
        TS = 32 * NS
        NKB = max(SEQ, PAST + 128) // 128
        L = DEPTH
        dr = P.dram
        I, O = "ExternalInput", "ExternalOutput"
        self.xp = dr("xp", [SEQ, D], F32, I)
        self.xs = dr("xs", [NS * 16, D], F32, I)
        self.memp = dr("memp", [256, D], F32, I)
        self.st_conv = dr("st_conv", [L, NS, 2, 512], F32, I)
        self.st_ret = dr("st_ret", [L, NS, 4, 128, 128], F32, I)
        self.c_k = dr("c_k", [L, NS, PAST, 512], F32, I)
        self.c_v = dr("c_v", [L, NS, PAST, 512], F32, I)
        self.c_mk = dr("c_mk", [L, NS, 256, 512], F32, I)
        self.c_mv = dr("c_mv", [L, NS, 256, 512], F32, I)
        self.w_up1 = dr("w_up1", [L, D, 2 * DFF], F32, I)
        self.w_dn1 = dr("w_dn1", [L, DFF, D], F32, I)
        self.w_up2 = dr("w_up2", [L, D, 2 * DFF], F32, I)
        self.w_dn2 = dr("w_dn2", [L, DFF, D], F32, I)
        self.w_in = dr("w_in", [L, D, 13 * 512], F32, I)
        self.w_mkv = dr("w_mkv", [L, D, 1024], F32, I)
        self.w_br = dr("w_br", [L * 4, 512, D], F32, I)
        self.w_gt = dr("w_gt", [L * 4, D, D], F32, I)
        self.w_o = dr("w_o", [L, D, D], F32, I)
        self.lnp_d = dr("lnp", [128, L * 6 * 16], F32, I)
        self.bg_d = dr("bgate", [128, L * 4 * 16], F32, I)
        self.cw_d = dr("convw", [128, L * 3 * 4], F32, I)
        self.gn_d = dr("gn", [L, 512], F32, I)
        self.sub_d = dr("subg", [L, 128], F32, I)
        self.lam_d = dr("lam", [L, 256], F32, I)
        self.rb_d = dr("relb", [1, 128], F32, I)
        self.rb2_d = dr("relb2", [32, 4], F32, I)
        self.rope_d = dr("rope", [NT + 1, 2, 128, T], F32, I)
        self.qd_d = dr("qd", [2, 128, 4 * T], BF16, I)
        self.rm_d = dr("rm", [2, 128, 512], F32, I)
        self.kdx_d = dr("kdx", [2, 128, 512], F32, I)
        self.cdec_d = dr("cdec", [2, 128, 4], F32, I)
        self.oh_d = dr("oh", [32, 2 * 128 * 128], F32, I)
        self.mask_d = dr("mask0", [128, 128], F32, I)
        self.id_d = dr("ident", [128, 128], F32, I)
        self.idb_d = dr("identb", [128, 128], BF16, I)
        self.on_d = dr("ones", [128, 128], BF16, I)
        self.NUNITS = 128 * L
        self.wscrs = [dr(f"wscr{i}", [64, 128, 16 * 512], BF16, "Internal") for i in range(self.NUNITS // 64)]
        self.tix = 0
        self.yp = dr("yp", [SEQ, D], F32, O)
        self.ys = dr("ys", [NS * 16, D], F32, O)
        self.o_conv_p = dr("o_conv_p", [L, 2, 512], F32, O)
        self.o_ret_p = dr("o_ret_p", [L, 4, 128, 128], F32, O)
        self.o_dk_p = dr("o_dk_p", [L, SEQ, 512], F32, O)
        self.o_dv_p = dr("o_dv_p", [L, SEQ, 512], F32, O)
        self.o_mk_p = dr("o_mk_p", [L, 256, 512], F32, O)
        self.o_mv_p = dr("o_mv_p", [L, 256, 512], F32, O)
        self.o_conv_s = dr("o_conv_s", [L, NS, 2, 512], F32, O)
        self.o_ret_s = dr("o_ret_s", [L, NS, 4, 128, 128], F32, O)
        self.o_dk_s = dr("o_dk_s", [L, NS, 16, 512], F32, O)
        self.o_dv_s = dr("o_dv_s", [L, NS, 16, 512], F32, O)
        import os as _os
        self.dbg = dr("dbg", [128, 4 * 64], BF16, O) if _os.environ.get("K_DBG") else None

        sb = P.sb
        self.banks = [P.ps(f"bank{i}", [128, 512], F32) for i in range(8)]
        self.NBUF = 3
        self.WR = sb("WR", [128, self.NBUF, 16, 512], BF16)
        self.X = sb("X", [128, DC, T], F32)
        self.X16 = sb("X16", [128, DC, T], BF16)
        self.tmps = [sb(f"tmp{i}", [128, 512], F32) for i in range(4)]

        self.YT = [sb(f"YT{b}", [128, 4, T], BF16) for b in range(4)]
        self.DQ = sb("DQ", [128, 4, T], BF16)
        self.MQ = sb("MQ", [128, 4, T], BF16)
        self.UC = [sb(f"UC{l}", [128, 4, 2], F32) for l in range(L)]
        self.SL = [sb(f"SL{l}", [128, 4, 128], F32) for l in range(L)]
        self.MKT = [sb(f"MKT{l}", [128, 4, 256], BF16) for l in range(L)]
        self.MEMV = [sb(f"MEMV{l}", [128, 2, 512], BF16) for l in range(L)]
        self.ident = sb("ident", [128, 128], F32)
        self.identb = sb("identb", [128, 128], BF16)
        self.ones = sb("ones", [128, 128], BF16)
        self.lnp = sb("lnp", [128, L * 6 * 16], F32)
        self.bg = sb("bg", [128, L * 4 * 16], F32)
        self.cw = sb("cw", [128, L * 3 * 4], F32)
        self.GN = sb("GN", [128, L, 512], BF16)
        self.SUBG = sb("SUBG", [128, L, 128], F32)
        self.LAMC = sb("LAMC", [128, L, 4], F32)
        self.BT = sb("BT", [128, 2, 4, 128], F32)
        self.CF = sb("CF", [128, 8], F32)
        self.ROPE = sb("ROPE", [128, 2, T], F32)
        self.QD = sb("QD", [128, 4, T], BF16)
        self.RM = sb("RM", [128, 4, 128], F32)
        self.KDX = sb("KDX", [128, 512], F32)
        self.CDEC = sb("CDEC", [128, 4], F32)
        self.stat = sb("stat", [128, 64], F32)
        self.S16 = [sb(f"S16_{i}", [128, 128], BF16) for i in range(2)]
        self.s16_i = 0

        q = "sp"
        for (dst, src) in ((self.ident, self.id_d), (self.identb, self.idb_d), (self.ones, self.on_d), (self.lnp, self.lnp_d),
                           (self.bg, self.bg_d), (self.cw, self.cw_d)):
            self.dma(q, V(dst), V(src))
        for l in range(L):
            self.dma("pool", V(self.GN, self.GN[:, l, :]), V(self.gn_d, self.gn_d[l:l + 1, :].partition_broadcast(128)))
            self.dma(q, V(self.SUBG, self.SUBG[:, l, :]), V(self.sub_d, self.sub_d[l:l + 1, :].partition_broadcast(128)))
        import os as _os
        STG0 = int(_os.environ.get("K_STAGE", "9"))
        if STG0 != -1:
            self.setup_consts()
        P.flush()

        KT_ = _os.environ.get("K_TILES", "ps")
        for t in range(NT + 1 if STG0 >= 0 else 0):
            sample = (t == NT)
            if sample and "s" not in KT_:
                continue
            if not sample and "p" not in KT_:
                continue
            if not sample and "1" in KT_ and t > 0:
                continue
            Tt = TS if sample else T
            self.tile_T = Tt
            self.sample = sample
            self.tix = t
            self.tbs = [(0, TS)] if sample else [(i * 128, 128) for i in range(T // 128)]
            self.load_tile(t)
            import os as _os
            STG = int(_os.environ.get("K_STAGE", "9"))
            for l in range(L):
                if t == 0 and STG >= 1:
                    self.mem_kv(l)
                wi = self.w_in
                up2 = self.w_up2
                n_mix = [(wi, l, 0, 16, 0), (wi, l, 0, 16, 512), (wi, l, 0, 16, 1024)]
                n_f2 = [(up2, l, 0, 16, 0), (up2, l, 0, 16, 512), (up2, l, 0, 16, 1024)]
                if l + 1 < L:
                    n_f1 = None if t == 0 else [(self.w_up1, l + 1, 0, 16, 0), (self.w_up1, l + 1, 0, 16, 512), (self.w_up1, l + 1, 0, 16, 1024)]
                else:
                    n_f1 = None
                if STG >= 2:
                    self.ffn(l, 0, n_mix)
                if STG >= 3:
                    self.mixer(l)
                if STG >= 6:
                    self.gates(l, n_f2)
                if STG >= 7:
                    self.ffn(l, 1, n_f1)
            if "n" not in KT_:
                self.store_tile(t)
        P.flush(final=True)
        P.close()
        return self.nc

    def setup_consts(self):
        P, L = self.P, self.cfg.DEPTH
        with P.scope():
            ohs = [P.sb(f"oh{i}", [32, 16 * 128], F32) for i in range(2)]
            rb2 = P.sb("rb2", [32, 4], F32)
            rbb = P.sb("rbb", [128, 128], F32)
            msk = P.sb("msk", [128, 128], F32)
            lamb = P.sb("lamb", [128, L, 256], F32)
            lt = P.sb("lt", [128, 256], F32)
            self.dma("sp", V(rb2), V(self.rb2_d))
            self.dma("sp", V(rbb), V(self.rb_d, self.rb_d[0:1, :].partition_broadcast(128)))
            self.dma("sp", V(msk), V(self.mask_d))
            for l in range(L):
                self.dma("sp", V(lamb, lamb[:, l, :]), V(self.lam_d, self.lam_d[l:l + 1, :].partition_broadcast(128)))
            for ti in range(2):
                bk = self.bank()
                for kg in range(8):
                    oh = ohs[kg % 2]
                    o0 = (ti * 128 + kg * 16) * 128
                    self.dma("sp", V(oh), V(self.oh_d, self.oh_d[:, o0:o0 + 16 * 128]))
                    for kk in range(16):
                        k = kg * 16 + kk
                        self.mm(V(bk, bk[:, k * 4:(k + 1) * 4]), V(oh, oh[:, kk * 128:(kk + 1) * 128]), V(rb2), True, True)
                for h in range(4):
                    src = bk[:].rearrange("p (k h) -> p h k", h=4)[:, h, :]
                    if ti == 0:
                        self.tt("dve", V(self.BT, self.BT[:, ti, h, :]), V(bk, src), V(msk), ALU.add)
                    else:
                        self.cp("dve", V(self.BT, self.BT[:, ti, h, :]), V(bk, src))
            self.cp("dve", V(self.CF, self.CF[:, 0:4]), V(rbb, rbb[:, 60:64]))
            self.red("dve", V(self.CF, self.CF[:, 4:8]), V(rbb, rbb[:].rearrange("p (b h) -> p h b", h=4)), ALU.max)
            for l in range(L):
                lam_init = 0.8 - 0.6 * math.exp(-0.3 * l)
                self.tt("dve", V(lt, lt[:, 0:64]), V(lamb, lamb[:, l, 0:64]), V(lamb, lamb[:, l, 64:128]), ALU.mult)
                self.tt("dve", V(lt, lt[:, 64:128]), V(lamb, lamb[:, l, 128:192]), V(lamb, lamb[:, l, 192:256]), ALU.mult)
                self.red("dve", V(self.stat, self.stat[:, 0:1]), V(lt, lt[:, 0:64]), ALU.add)
                self.red("dve", V(self.stat, self.stat[:, 1:2]), V(lt, lt[:, 64:128]), ALU.add)
                self.act(V(self.stat, self.stat[:, 2:4]), V(self.stat, self.stat[:, 0:2]), AF.Exp)
                self.tt("dve", V(self.stat, self.stat[:, 4:5]), V(self.stat, self.stat[:, 2:3]), V(self.stat, self.stat[:, 3:4]), ALU.subtract)
                self.ts("dve", V(self.LAMC, self.LAMC[:, l, 0:1]), V(self.stat, self.stat[:, 4:5]), lam_init, None, ALU.add)
                self.ts("dve", V(self.LAMC, self.LAMC[:, l, 1:2]), V(self.LAMC, self.LAMC[:, l, 0:1]), -1.0, None, ALU.mult)

    def load_tile(self, t):
        P, cfg = self.P, self.cfg
        T, NS = cfg.T, cfg.NS
        kind = 1 if self.sample else 0
        Tt = self.tile_T
        with P.scope():
            stg = P.sb("stg", [128, D], F32)
            import os as _os
            SK = _os.environ.get("K_SKIP", "")
            if "r" not in SK:
                self.dma("sp", V(self.ROPE), V(self.rope_d, self.rope_d[t].rearrange("a p t -> p a t")))
            if "q" not in SK:
                self.dma("sp", V(self.QD), V(self.qd_d, self.qd_d[kind].rearrange("p (h t) -> p h t", h=4)))
                self.dma("sp", V(self.RM), V(self.rm_d, self.rm_d[kind].rearrange("p (h t) -> p h t", h=4)))
            if "k" not in SK:
                self.dma("sp", V(self.KDX), V(self.kdx_d, self.kdx_d[kind]))
                self.dma("sp", V(self.CDEC), V(self.cdec_d, self.cdec_d[kind]))
            for (c0, nr) in (self.tbs if "x" not in SK else []):
                if self.sample:
                    self.memset("dve", V(stg), 0.0)
                    for s in range(NS):
                        self.dma("sp", V(stg, stg[32 * s:32 * s + 16, :]), V(self.xs, self.xs[16 * s:16 * s + 16, :]))
                else:
                    self.dma("sp", V(stg), V(self.xp, self.xp[t * T + c0:t * T + c0 + nr, :]))
                for g in range(4):
                    bk = self.bank()
                    for j in range(4):
                        c = g * 4 + j
                        self.mm(V(bk, bk[:, j * 128:j * 128 + nr]), V(stg, stg[0:nr, c * 128:(c + 1) * 128]),
                                V(self.ident, self.ident[0:nr, 0:nr]))
                    for j in range(4):
                        c = g * 4 + j
                        if "d" not in SK:
                            self.cp("dve", V(self.X, self.X[:, c, c0:c0 + nr], c, c + 1), V(bk, bk[:, j * 128:j * 128 + nr]))
                        if "a" not in SK:
                            self.cp("dve", V(self.X16, self.X16[:, c, c0:c0 + nr], c, c + 1), V(self.X, self.X[:, c, c0:c0 + nr], c, c + 1))

    def store_tile(self, t):
        P, cfg = self.P, self.cfg
        T, NS = cfg.T, cfg.NS
        with P.scope():
            stg = P.sb("stgo", [128, D], F32)
            for (c0, nr) in self.tbs:
                for g in range(4):
                    bk = self.bank()
                    for j in range(4):
                        c = g * 4 + j
                        self.mm(V(bk, bk[0:nr, j * 128:(j + 1) * 128]), V(self.X, self.X[:, c, c0:c0 + nr], c, c + 1), V(self.ident))
                    self.cp("act" if g % 2 else "dve", V(stg, stg[0:nr, g * 512:(g + 1) * 512], g, g + 1), V(bk, bk[0:nr, :]))
                if self.sample:
                    for s in range(NS):
                        self.dma("sp", V(self.ys, self.ys[16 * s:16 * s + 16, :]), V(stg, stg[32 * s:32 * s + 16, :]))
                else:
                    self.dma("sp", V(self.yp, self.yp[t * T + c0:t * T + c0 + nr, :]), V(stg, stg[0:nr, :]))

    def layernorm(self, l, which, SQ):
        Tt = self.tile_T
        X, X16 = self.X, self.X16
        for c in range(DC):
            self.cp("act", V(X16, X16[:, c, 0:Tt], c, c + 1), V(X, X[:, c, 0:Tt], c, c + 1))
            self.tt("pool", V(SQ, SQ[:, c, 0:Tt], c, c + 1), V(X, X[:, c, 0:Tt], c, c + 1), V(X, X[:, c, 0:Tt], c, c + 1), ALU.mult)
        bk = self.bank()
        bk2 = self.bank()
        for c in range(DC):
            self.mm(V(bk, bk[:, 0:Tt]), V(self.ones), V(X16, X16[:, c, 0:Tt], c, c + 1), c == 0, c == DC - 1)
        for c in range(DC):
            self.mm(V(bk2, bk2[:, 0:Tt]), V(self.ones), V(SQ, SQ[:, c, 0:Tt], c, c + 1), c == 0, c == DC - 1)
        t1, t2, t3, t4 = self.tmp(), self.tmp(), self.tmp(), self.tmp()
        mean = V(t1, t1[:, 0:Tt])
        self.cp("dve", mean, V(bk, bk[:, 0:Tt]))
        self.tt("dve", V(t2, t2[:, 0:Tt]), mean, mean, ALU.mult)
        self.tt("dve", V(t2, t2[:, 0:Tt]), V(bk2, bk2[:, 0:Tt]), V(t2, t2[:, 0:Tt]), ALU.subtract)
        self.ts("dve", V(t2, t2[:, 0:Tt]), V(t2, t2[:, 0:Tt]), 0.0, None, ALU.max)
        self.act(V(t2, t2[:, 0:Tt]), V(t2, t2[:, 0:Tt]), AF.Sqrt, bias=LN_EPS, scale=1.0)
        rstd = V(t3, t3[:, 0:Tt])
        self.recip(rstd, V(t2, t2[:, 0:Tt]))
        nmr = V(t4, t4[:, 0:Tt])
        self.stt("dve", nmr, mean, -1.0, rstd, ALU.mult, ALU.mult)
        gi = (l * 6 + 2 * which) * 16
        bi = (l * 6 + 2 * which + 1) * 16
        for c in range(DC):
            xc = V(X, X[:, c, 0:Tt], c, c + 1)
            self.tt("dve", xc, xc, rstd, ALU.mult)
            self.tt("pool", xc, xc, nmr, ALU.add)
            self.act(xc, xc, AF.Identity, bias=V(self.lnp, self.lnp[:, bi + c:bi + c + 1]), scale=V(self.lnp, self.lnp[:, gi + c:gi + c + 1]))
            self.cp("act", V(X16, X16[:, c, 0:Tt], c, c + 1), xc)

    def ffn(self, l, which, nxt=None):
        P = self.P
        Tt = self.tile_T
        wu = self.w_up1 if which == 0 else self.w_up2
        wd = self.w_dn1 if which == 0 else self.w_dn2
        X, X16 = self.X, self.X16
        with P.scope():
            HID = P.sb("HID", [128, FC, self.cfg.T], BF16)
            for g in range(22):
                sa = self.wload(wu, l, 0, 16, g * 512)
                for j in range(2):
                    bk = self.bank()
                    bkb = self.bank()
                    for k in range(DC):
                        self.mm(V(bk, bk[:, 0:Tt]), self.W(sa, k, j * 128, (j + 1) * 128), V(X16, X16[:, k, 0:Tt], k, k + 1), k == 0, k == DC - 1)
                    for k in range(DC):
                        self.mm(V(bkb, bkb[:, 0:Tt]), self.W(sa, k, 256 + j * 128, 256 + (j + 1) * 128), V(X16, X16[:, k, 0:Tt], k, k + 1), k == 0, k == DC - 1)
                    tm = self.tmp()
                    self.act(V(tm, tm[:, 0:Tt]), V(bk, bk[:, 0:Tt]), AF.Silu)
                    f = g * 2 + j
                    self.stt("dve", V(HID, HID[:, f, 0:Tt], f, f + 1), V(tm, tm[:, 0:Tt]), 0.5, V(bkb, bkb[:, 0:Tt]), ALU.mult, ALU.mult)
            for cg in range(4):
                bks = [self.bank(), self.bank(), self.bank(), self.bank()]
                for sub in range(4):
                    s = self.wload(wd, l, sub * 11 * 128, 11, cg * 512)
                    for oc in range(4):
                        bk = bks[oc]
                        o = V(bk, bk[:, 0:Tt])
                        for k in range(11):
                            f = sub * 11 + k
                            self.mm(o, self.W(s, k, oc * 128, (oc + 1) * 128), V(HID, HID[:, f, 0:Tt], f, f + 1),
                                    sub == 0 and k == 0, sub == 3 and k == 10)
                for oc in range(4):
                    c = cg * 4 + oc
                    bk = bks[oc]
                    xc = V(X, X[:, c, 0:Tt], c, c + 1)
                    self.stt("dve", xc, xc, self.cfg.ALPHA, V(bk, bk[:, 0:Tt]), ALU.mult, ALU.add)
            self.prefetch(nxt)
            self.layernorm(l, 0 if which == 0 else 2, HID)

    def mem_kv(self, l):
        P = self.P
        with P.scope():
            stg = P.sb("mstg", [128, D], F32)
            memT = P.sb("memT", [128, DC, 256], BF16)
            kf = P.sb("mkf", [128, 512], F32)
            for tb in range(2):
                self.dma("sp", V(stg), V(self.memp, self.memp[tb * 128:(tb + 1) * 128, :]))
                for g in range(4):
                    bk = self.bank()
                    for j in range(4):
                        c = g * 4 + j
                        self.mm(V(bk, bk[:, j * 128:(j + 1) * 128]), V(stg, stg[:, c * 128:(c + 1) * 128]), V(self.ident))
                    self.cp("act", V(memT, memT[:, g * 4:g * 4 + 4, tb * 128:(tb + 1) * 128]), V(bk, bk[:].rearrange("p (j n) -> p j n", j=4)))
            for kv in range(2):
                s = self.wload(self.w_mkv, l, 0, 16, kv * 512, cache=False)
                for tb in range(2):
                    bk = self.bank()
                    for k in range(DC):
                        self.mm(V(bk), V(memT, memT[:, k, tb * 128:(tb + 1) * 128]), self.W(s, k, 0, 512), k == 0, k == DC - 1)
                    self.cp("act", V(kf), V(bk))
                    dst = self.o_mk_p if kv == 0 else self.o_mv_p
                    self.dma("sp", V(dst, dst[l, tb * 128:(tb + 1) * 128, :]), V(kf))
                    if kv == 1:
                        self.cp("dve", V(self.MEMV[l], self.MEMV[l][:, tb, :]), V(bk))
                if kv == 0:
                    for hp in range(2):
                        bk = self.bank()
                        for hh in range(2):
                            h = hp * 2 + hh
                            for k in range(DC):
                                self.mm(V(bk, bk[:, hh * 256:(hh + 1) * 256]), self.W(s, k, h * 128, (h + 1) * 128), V(memT, memT[:, k, :]), k == 0, k == DC - 1)
                        self.cp("act", V(self.MKT[l], self.MKT[l][:, hp * 2:hp * 2 + 2, :]), V(bk, bk[:].rearrange("p (a n) -> p a n", a=2)))

    def proj_fm(self, s, j, bkv):
        Tt = self.tile_T
        for k in range(DC):
            self.mm(bkv, self.W(s, k, j * 128, (j + 1) * 128), V(self.X16, self.X16[:, k, 0:Tt], k, k + 1), k == 0, k == DC - 1)

    def proj_tm(self, s, c0, nr, bk):
        for k in range(DC):
            self.mm(V(bk, bk[0:nr, :]), V(self.X16, self.X16[:, k, c0:c0 + nr], k, k + 1), self.W(s, k, 0, 512), k == 0, k == DC - 1)

    def transpose_to_fm(self, src, dstT):
        for bi, (c0, nr) in enumerate(self.tbs):
            bk = self.bank()
            for h in range(4):
                self.mm(V(bk, bk[:, h * 128:h * 128 + nr]), V(src, src[0:nr, bi, h * 128:(h + 1) * 128]), V(self.identb, self.identb[0:nr, 0:nr]))
            self.cp("act", V(dstT, dstT[:, :, c0:c0 + nr]), V(bk, bk[:].rearrange("p (h n) -> p h n", h=4)[:, :, 0:nr]))

    def mixer(self, l):
        P, cfg = self.P, self.cfg
        T, NS, PAST, SEQ = cfg.T, cfg.NS, cfg.PAST, cfg.SEQ
        Tt = self.tile_T
        t = self.tix
        sample = self.sample
        ntb = len(self.tbs)
        X16 = self.X16
        win = self.w_in
        YAT, YBT, YCT, YDT = self.YT
        with P.scope():
            sb = P.sb
            CB = sb("CB", [128, 4, T], BF16)
            CC = sb("CC", [128, 4, T], F32)
            st = self.stat
            if sample:
                U = sb("US", [128, 4, NS * 18], F32)
                segs = [(32 * s, 16, 18 * s) for s in range(NS)]
                for s in range(NS):
                    for j in range(4):
                        self.dma("sp", V(U, U[:, j, 18 * s:18 * s + 2]),
                                 V(self.st_conv, self.st_conv[l, s, :, j * 128:(j + 1) * 128].rearrange("t p -> p t")), slow=True)
            else:
                U = sb("UP", [128, 4, 2 + T], F32)
                segs = [(0, T, 0)]
                if t == 0:
                    self.memset("dve", V(U, U[:, :, 0:2]), 0.0)
                else:
                    self.cp("dve", V(U, U[:, :, 0:2]), V(self.UC[l]))
            s0 = self.wload(win, l, 0, 16, 0)
            for j in range(4):
                bk = self.bank()
                self.proj_fm(s0, j, V(bk, bk[:, 0:Tt]))
                self.cp("dve", V(CB, CB[:, j, 0:Tt], j, j + 1), V(bk, bk[:, 0:Tt]))
            s1 = self.wload(win, l, 0, 16, 512)
            for j in range(4):
                bk = self.bank()
                self.proj_fm(s1, j, V(bk, bk[:, 0:Tt]))
                self.cp("dve", V(CC, CC[:, j, 0:Tt], j, j + 1), V(bk, bk[:, 0:Tt]))
            s2 = self.wload(win, l, 0, 16, 1024)
            for j in range(4):
                bk = self.bank()
                self.proj_fm(s2, j, V(bk, bk[:, 0:Tt]))
                for (c0, Ls, u0) in segs:
                    self.tt("dve", V(U, U[:, j, u0 + 2:u0 + 2 + Ls]), V(CC, CC[:, j, c0:c0 + Ls], j, j + 1), V(bk, bk[:, c0:c0 + Ls]), ALU.mult)
            for j in range(4):
                for (c0, Ls, u0) in segs:
                    tm = self.tmp()
                    z = V(tm, tm[:, 0:Ls])
                    wi = (l * 3) * 4
                    self.ts("dve", z, V(U, U[:, j, u0:u0 + Ls]), V(self.cw, self.cw[:, wi + j:wi + j + 1]), None, ALU.mult)
                    self.stt("dve", z, V(U, U[:, j, u0 + 1:u0 + 1 + Ls]), V(self.cw, self.cw[:, wi + 4 + j:wi + 4 + j + 1]), z, ALU.mult, ALU.add)
                    self.stt("dve", z, V(U, U[:, j, u0 + 2:u0 + 2 + Ls]), V(self.cw, self.cw[:, wi + 8 + j:wi + 8 + j + 1]), z, ALU.mult, ALU.add)
                    self.tt("dve", V(YAT, YAT[:, j, c0:c0 + Ls]), z, V(CB, CB[:, j, c0:c0 + Ls]), ALU.mult)
            if sample:
                self.memset("dve", V(YAT, YAT[:, :, 16:32]), 0.0)
                if NS > 1:
                    self.memset("dve", V(YAT, YAT[:, :, 48:64]), 0.0)
            if sample:
                for s in range(NS):
                    for j in range(4):
                        self.dma("sp", V(self.o_conv_s, self.o_conv_s[l, s, :, j * 128:(j + 1) * 128].rearrange("t p -> p t")),
                                 V(U, U[:, j, 18 * s + 16:18 * s + 18]), slow=True)
            else:
                if t == cfg.NT - 1:
                    for j in range(4):
                        self.dma("sp", V(self.o_conv_p, self.o_conv_p[l, :, j * 128:(j + 1) * 128].rearrange("t p -> p t")),
                                 V(U, U[:, j, T:T + 2]), slow=True)
                else:
                    self.cp("dve", V(self.UC[l]), V(U, U[:, :, T:T + 2]))
            self.prefetch([(win, l, 0, 16, 3 * 512), (win, l, 0, 16, 4 * 512), (win, l, 0, 16, 5 * 512)])
        with P.scope():
            sb = P.sb
            QT = sb("QT", [128, 4, T], BF16)
            KT = sb("KT", [128, 4, T], BF16)
            KH = sb("KH", [128, ntb, 512], BF16)
            VR = sb("VR", [128, ntb, 512], BF16)
            SG = sb("SG", [128, ntb, 512], BF16)
            ATM = sb("ATM", [128, 4, 128], BF16)
            OS = sb("OS", [128, 512], F32)
            YTOK = sb("YTOK", [128, ntb, 512], BF16)
            st = self.stat
            for (ua, ub, dst, isq) in ((3, 4, QT, True), (5, 6, KT, False)):
                sa = self.wload(win, l, 0, 16, ua * 512)
                sw = self.wload(win, l, 0, 16, ub * 512)
                for h in range(4):
                    bka, bkb = self.bank(), self.bank()
                    self.proj_fm(sa, h, V(bka, bka[:, 0:Tt]))
                    self.proj_fm(sw, h, V(bkb, bkb[:, 0:Tt]))
                    if True:
                        t1, t2 = self.tmp(), self.tmp()
                        a = V(t1, t1[:, 0:Tt]); b = V(t2, t2[:, 0:Tt])
                        self.tt("dve", a, V(bka, bka[:, 0:Tt]), V(self.ROPE, self.ROPE[:, 0, 0:Tt]), ALU.mult)
                        self.tt("dve", b, V(bkb, bkb[:, 0:Tt]), V(self.ROPE, self.ROPE[:, 1, 0:Tt]), ALU.mult)
                        if isq:
                            self.tt("dve", a, a, b, ALU.add)
                            self.tt("dve", V(dst, dst[:, h, 0:Tt]), a, V(self.QD, self.QD[:, h, 0:Tt]), ALU.mult)
                        else:
                            self.tt("dve", V(dst, dst[:, h, 0:Tt]), a, b, ALU.add)
            for bi, (c0, nr) in enumerate(self.tbs):
                bk = self.bank()
                for h in range(4):
                    self.mm(V(bk, bk[0:nr, h * 128:(h + 1) * 128]), V(KT, KT[:, h, c0:c0 + nr]), V(self.identb))
                self.tt("dve", V(KH, KH[0:nr, bi, :]), V(bk, bk[0:nr, :]), V(self.KDX, self.KDX[0:nr, :]), ALU.mult)
            s7 = self.wload(win, l, 0, 16, 7 * 512)
            for bi, (c0, nr) in enumerate(self.tbs):
                bk = self.bank()
                self.proj_tm(s7, c0, nr, bk)
                self.cp("act", V(VR, VR[0:nr, bi, :]), V(bk, bk[0:nr, :]))
            s8 = self.wload(win, l, 0, 16, 8 * 512)
            for bi, (c0, nr) in enumerate(self.tbs):
                bk = self.bank()
                self.proj_tm(s8, c0, nr, bk)
                self.act(V(SG, SG[0:nr, bi, :]), V(bk, bk[0:nr, :]), AF.Silu)
            if sample:
                SS = [sb(f"SS{s}", [128, 4, 128], F32) for s in range(NS)]
                for s in range(NS):
                    self.dma("sp", V(SS[s]), V(self.st_ret, self.st_ret[l, s].rearrange("h d e -> d h e")))
            else:
                if t == 0:
                    self.memset("dve", V(self.SL[l]), 0.0)
            for bi, (c0, nr) in enumerate(self.tbs):
                if sample:
                    chunks = [(32 * s, 32, 32 * s, SS[s]) for s in range(NS)]
                else:
                    chunks = [(0, 64, c0, self.SL[l]), (64, 64, c0 + 64, self.SL[l])]
                bkA = self.bank()
                for h in range(4):
                    self.mm(V(bkA, bkA[0:nr, h * 128:h * 128 + nr]), V(KT, KT[:, h, c0:c0 + nr]), V(QT, QT[:, h, c0:c0 + nr]))
                self.tt("dve", V(ATM, ATM[0:nr, :, 0:nr]), V(bkA, bkA[0:nr, :].rearrange("p (h n) -> p h n", h=4)[:, :, 0:nr]),
                        V(self.RM, self.RM[0:nr, :, 0:nr]), ALU.mult)
                bkO = self.bankO()
                for h in range(4):
                    for ci, (p0, ln, cc0, ST) in enumerate(chunks):
                        s16 = self.S16[self.s16_i % 2]; self.s16_i += 1
                        self.cp("act", V(s16), V(ST, ST[:, h, :], h, h + 1))
                        self.mm(V(bkO, bkO[p0:p0 + ln, h * 128:(h + 1) * 128]), V(ATM, ATM[0:nr, h, p0:p0 + ln]), V(VR, VR[0:nr, bi, h * 128:(h + 1) * 128]), True, False)
                        self.mm(V(bkO, bkO[p0:p0 + ln, h * 128:(h + 1) * 128]), V(QT, QT[:, h, cc0:cc0 + ln]), V(s16), False, True)
                        bkS = self.bank()
                        self.mm(V(bkS, bkS[:, 0:128]), V(KH, KH[p0:p0 + ln, bi, h * 128:(h + 1) * 128]), V(VR, VR[p0:p0 + ln, bi, h * 128:(h + 1) * 128]))
                        self.stt("dve", V(ST, ST[:, h, :], h, h + 1), V(ST, ST[:, h, :], h, h + 1), V(self.CDEC, self.CDEC[:, h:h + 1]), V(bkS, bkS[:, 0:128]), ALU.mult, ALU.add)
                self.cp("act", V(OS, OS[0:nr, :]), V(bkO, bkO[0:nr, :]))
                self.red("dve", V(st, st[0:nr, 0:4]), V(OS, OS[0:nr, :].rearrange("p (h e) -> p h e", h=4)), ALU.add)
                self.ts("dve", V(st, st[0:nr, 4:8]), V(st, st[0:nr, 0:4]), -1.0 / 128, None, ALU.mult)
                self.memset("dve", V(st, st[0:nr, 8:12]), 0.0)
                for h in range(4):
                    tm = self.tmp()
                    self.act(V(tm, tm[0:nr, 0:128]), V(OS, OS[0:nr, h * 128:(h + 1) * 128]), AF.Square, bias=V(st, st[0:nr, 4 + h:5 + h]), scale=1.0,
                             accum=V(st, st[0:nr, 8 + h:9 + h]))
                self.act(V(st, st[0:nr, 12:16]), V(st, st[0:nr, 8:12]), AF.Sqrt, bias=LN_EPS, scale=1.0 / 128)
                self.recip(V(st, st[0:nr, 16:20]), V(st, st[0:nr, 12:16]))
                for h in range(4):
                    self.ts("dve", V(OS, OS[0:nr, h * 128:(h + 1) * 128]), V(OS, OS[0:nr, h * 128:(h + 1) * 128]), V(st, st[0:nr, 4 + h:5 + h]),
                            V(st, st[0:nr, 16 + h:17 + h]), ALU.add, ALU.mult)
                self.tt("dve", V(OS, OS[0:nr, :]), V(OS, OS[0:nr, :]), V(self.GN, self.GN[0:nr, l, :]), ALU.mult)
                self.tt("dve", V(YTOK, YTOK[0:nr, bi, :]), V(OS, OS[0:nr, :]), V(SG, SG[0:nr, bi, :]), ALU.mult)
            self.transpose_to_fm(YTOK, YBT)
            if sample:
                for s in range(NS):
                    self.dma("sp", V(self.o_ret_s, self.o_ret_s[l, s].rearrange("h d e -> d h e")), V(SS[s]))
            elif t == cfg.NT - 1:
                self.dma("sp", V(self.o_ret_p, self.o_ret_p[l].rearrange("h d e -> d h e")), V(self.SL[l]))
            self.prefetch([(win, l, 0, 16, 9 * 512), (win, l, 0, 16, 10 * 512), (win, l, 0, 16, 11 * 512)])
        with P.scope():
            sb = P.sb
            KF0 = sb("KF0", [128, 512], F32)
            KF1 = sb("KF1", [128, 512], F32)
            KF = [KF0, KF1]
            s9 = self.wload(win, l, 0, 16, 9 * 512)
            for h in range(4):
                bk = self.bank()
                self.proj_fm(s9, h, V(bk, bk[:, 0:Tt]))
                self.ts("dve", V(self.DQ, self.DQ[:, h, 0:Tt]), V(bk, bk[:, 0:Tt]), 0.125, None, ALU.mult)
            for (u, dst_p, dst_s) in ((10, self.o_dk_p, self.o_dk_s), (11, self.o_dv_p, self.o_dv_s)):
                su = self.wload(win, l, 0, 16, u * 512)
                for bi, (c0, nr) in enumerate(self.tbs):
                    bk = self.bank()
                    self.proj_tm(su, c0, nr, bk)
                    kf = KF[(u + bi) % 2]
                    self.cp("act", V(kf, kf[0:nr, :]), V(bk, bk[0:nr, :]))
                    if sample:
                        for s in range(NS):
                            self.dma("sp", V(dst_s, dst_s[l, s]), V(kf, kf[32 * s:32 * s + 16, :]))
                    else:
                        r0 = t * T + c0
                        self.dma("sp", V(dst_p, dst_p[l, r0:r0 + nr, :], r0, r0 + nr), V(kf, kf[0:nr, :]))
            s12 = self.wload(win, l, 0, 16, 12 * 512)
            for h in range(4):
                bk = self.bank()
                self.proj_fm(s12, h, V(bk, bk[:, 0:Tt]))
                self.cp("dve", V(self.MQ, self.MQ[:, h, 0:Tt]), V(bk, bk[:, 0:Tt]))
            self.prefetch([(self.w_gt, l * 4, 0, 16, 0), (self.w_br, l * 4, 0, 4, 0), (self.w_gt, l * 4 + 1, 0, 16, 0)])
        import os as _os
        STG = int(_os.environ.get("K_STAGE", "9"))
        if STG >= 4:
            self.mem_attn(l)
        if STG >= 5:
            self.diff_attn(l)
            if self.dbg is not None and self.sample and l == 0:
                self.dma("sp", V(self.dbg, self.dbg.t.rearrange("p (h n) -> p h n", h=4)), V(self.YT[2], self.YT[2][:, :, 0:64]))
                self.P.flush()

    def mem_attn(self, l):
        P, cfg = self.P, self.cfg
        NS = cfg.NS
        sample = self.sample
        ntb = len(self.tbs)
        st = self.stat
        SC = 128 ** -0.5
        with P.scope():
            sb = P.sb
            PM = sb("PM", [128, 4, 256], BF16)
            PMT = sb("PMT", [128, 8, 128], BF16)
            YTOK = sb("YTOKd", [128, ntb, 512], BF16)
            if sample:
                MK = sb("MKs", [128, 2, 512], BF16)
                MKTs = sb("MKTs", [128, 4, 256], BF16)
                MVs = sb("MVs", [128, 2, 512], BF16)
                self.memset("dve", V(self.YT[3]), 0.0)
                qblocks = [(0, 32 * s, 16, s) for s in range(NS)]
            else:
                qblocks = [(bi, c0, nr, None) for bi, (c0, nr) in enumerate(self.tbs)]
            for (bi, c0, nq, s) in qblocks:
                if sample:
                    self.dma("pool", V(MK), V(self.c_mk, self.c_mk[l, s].rearrange("(b p) n -> p b n", p=128)))
                    self.dma("pool", V(MVs), V(self.c_mv, self.c_mv[l, s].rearrange("(b p) n -> p b n", p=128)))
                    for hp in range(2):
                        bk = self.bank()
                        for hh in range(2):
                            for kb in range(2):
                                self.mm(V(bk, bk[:, hh * 256 + kb * 128:hh * 256 + (kb + 1) * 128]), V(MK, MK[:, kb, (hp * 2 + hh) * 128:(hp * 2 + hh + 1) * 128]), V(self.identb))
                        self.cp("act", V(MKTs, MKTs[:, hp * 2:hp * 2 + 2, :]), V(bk, bk[:].rearrange("p (a n) -> p a n", a=2)))
                    mkt, mv = MKTs, MVs
                    prow = 32 * s
                else:
                    mkt, mv = self.MKT[l], self.MEMV[l]
                    prow = 0
                self.memset("dve", V(st, st[0:nq, 28:32]), 0.0)
                for hp in range(2):
                    bk = self.bank()
                    for hh in range(2):
                        h = hp * 2 + hh
                        self.mm(V(bk, bk[0:nq, hh * 256:(hh + 1) * 256]), V(self.MQ, self.MQ[:, h, c0:c0 + nq]), V(mkt, mkt[:, h, :]))
                    self.red("dve", V(st, st[0:nq, 20 + hp * 2:22 + hp * 2]), V(bk, bk[0:nq, :].rearrange("p (a n) -> p a n", a=2)), ALU.max)
                    self.ts("dve", V(st, st[0:nq, 24 + hp * 2:26 + hp * 2]), V(st, st[0:nq, 20 + hp * 2:22 + hp * 2]), -SC, None, ALU.mult)
                    for hh in range(2):
                        h = hp * 2 + hh
                        self.act(V(PM, PM[0:nq, h, :]), V(bk, bk[0:nq, hh * 256:(hh + 1) * 256]), AF.Exp, bias=V(st, st[0:nq, 24 + h:25 + h]), scale=SC,
                                 accum=V(st, st[0:nq, 28 + h:29 + h]))
                self.recip(V(st, st[0:nq, 32:36]), V(st, st[0:nq, 28:32]))
                for hp in range(2):
                    bk = self.bank()
                    for hh in range(2):
                        for kb in range(2):
                            h = hp * 2 + hh
                            self.mm(V(bk, bk[:, (hh * 2 + kb) * 128:(hh * 2 + kb) * 128 + nq]), V(PM, PM[0:nq, h, kb * 128:(kb + 1) * 128]), V(self.identb, self.identb[0:nq, 0:nq]))
                    self.cp("act", V(PMT, PMT[:, hp * 4:hp * 4 + 4, 0:nq]), V(bk, bk[:].rearrange("p (a n) -> p a n", a=4)[:, :, 0:nq]))
                bkO = self.bankO()
                for h in range(4):
                    for kb in range(2):
                        self.mm(V(bkO, bkO[0:nq, h * 128:(h + 1) * 128]), V(PMT, PMT[:, h * 2 + kb, 0:nq]), V(mv, mv[:, kb, h * 128:(h + 1) * 128]), kb == 0, kb == 1)
                for h in range(4):
                    self.ts("dve", V(YTOK, YTOK[0:nq, 0, h * 128:(h + 1) * 128]), V(bkO, bkO[0:nq, h * 128:(h + 1) * 128]), V(st, st[0:nq, 32 + h:33 + h]), None, ALU.mult)
                bk = self.bank()
                for h in range(4):
                    self.mm(V(bk, bk[:, h * 128:h * 128 + nq]), V(YTOK, YTOK[0:nq, 0, h * 128:(h + 1) * 128]), V(self.identb, self.identb[0:nq, 0:nq]))
                self.cp("act", V(self.YT[3], self.YT[3][:, :, c0:c0 + nq]), V(bk, bk[:].rearrange("p (h n) -> p h n", h=4)[:, :, 0:nq]))

    def diff_attn(self, l):
        P, cfg = self.P, self.cfg
        T, NS, PAST, SEQ = cfg.T, cfg.NS, cfg.PAST, cfg.SEQ
        t = self.tix
        sample = self.sample
        ntb = len(self.tbs)
        st = self.stat
        lam_init = 0.8 - 0.6 * math.exp(-0.3 * l)
        with P.scope():
            sb = P.sb
            if sample:
                nkb_max = PAST // 128 + 1
            else:
                nkb_max = (t + 1) * T // 128
            NK = nkb_max * 128
            KTOK = [sb(f"KTOK{i}", [128, 8, 128], BF16) for i in range(2)]
            KTh = sb("KTh", [128, NK], BF16)
            VH = sb("VH", [128, nkb_max, 128], BF16)
            P1 = sb("P1", [128, NK], BF16)
            P2 = sb("P2", [128, NK], BF16)
            AT = sb("AT", [128, nkb_max, 128], BF16)
            YTOK = sb("YTOKc", [128, 1, 128], BF16)
            NB = sb("NB", [128, 128], F32)
            MXS = sb("MXS", [128, 2, 16], F32)
            SMS = sb("SMS", [128, 2, 16], F32)
            OSc = sb("OSc", [128, 128], F32)
            if sample:
                self.memset("dve", V(self.YT[2]), 0.0)
            kt_i = 0
            seqs = list(range(NS)) if sample else [None]
            for sq in seqs:
                for h in range(4):
                    if sample:
                        ksrc, vsrc = self.c_k, self.c_v
                        kfull = lambda r0, r1: self.c_k[l, sq, r0:r1, h * 128:(h + 1) * 128]
                        vfull = lambda r0, r1: self.c_v[l, sq, r0:r1, h * 128:(h + 1) * 128]
                        nA = PAST // 128
                    else:
                        ksrc, vsrc = self.o_dk_p, self.o_dv_p
                        kfull = lambda r0, r1: self.o_dk_p[l, r0:r1, h * 128:(h + 1) * 128]
                        vfull = lambda r0, r1: self.o_dv_p[l, r0:r1, h * 128:(h + 1) * 128]
                        nA = nkb_max
                    for g0 in range(0, nA, 8):
                        n = min(8, nA - g0)
                        kt = KTOK[kt_i % 2]; kt_i += 1
                        self.dma("pool", V(kt, kt[:, 0:n, :]), V(ksrc, kfull(g0 * 128, (g0 + n) * 128).rearrange("(b p) n -> p b n", p=128)))
                        self.dma("pool", V(VH, VH[:, g0:g0 + n, :], g0, g0 + n), V(vsrc, vfull(g0 * 128, (g0 + n) * 128).rearrange("(b p) n -> p b n", p=128)))
                        for q0 in range(0, n, 4):
                            m4 = min(4, n - q0)
                            bk = self.bank()
                            for gi in range(m4):
                                self.mm(V(bk, bk[:, gi * 128:(gi + 1) * 128]), V(kt, kt[:, q0 + gi, :]), V(self.identb))
                            c0k = (g0 + q0) * 128
                            self.cp("dve", V(KTh, KTh[:, c0k:c0k + m4 * 128]), V(bk, bk[:, 0:m4 * 128]))
                    if sample:
                        kt = KTOK[kt_i % 2]; kt_i += 1
                        self.dma("pool", V(kt, kt[0:16, 0, :]), V(self.o_dk_s, self.o_dk_s[l, sq, :, h * 128:(h + 1) * 128]))
                        self.dma("pool", V(VH, VH[0:16, nA, :], nA, nA + 1), V(self.o_dv_s, self.o_dv_s[l, sq, :, h * 128:(h + 1) * 128]))
                        bk = self.bank()
                        self.mm(V(bk, bk[:, 0:16]), V(kt, kt[0:16, 0, :]), V(self.identb, self.identb[0:16, 0:16]))
                        self.cp("dve", V(KTh, KTh[:, nA * 128:nA * 128 + 16]), V(bk, bk[:, 0:16]))
                    if sample:
                        qbs = [(0, 32 * sq, 16, None, 32 * sq)]
                    else:
                        qbs = [(bi, c0, nr, (t * T + c0) // 128, 0) for bi, (c0, nr) in enumerate(self.tbs)]
                    for (bi, c0, nq, gb, prow) in qbs:
                        if sample:
                            nfar = PAST // 128 - 1
                            pieces = [(c, min(512, nfar * 128 - c), None) for c in range(0, nfar * 128, 512)]
                            pieces.append((nfar * 128, 128, 1))
                            pieces.append((PAST, 16, 0))
                            nk = PAST + 16
                        else:
                            nfar = max(gb - 1, 0)
                            pieces = [(c, min(512, nfar * 128 - c), None) for c in range(0, nfar * 128, 512)]
                            if gb >= 1:
                                pieces.append(((gb - 1) * 128, 128, 1))
                            pieces.append((gb * 128, 128, 0))
                            nk = (gb + 1) * 128
                        npc = len(pieces)
                        assert npc <= 16
                        for c in range(2):
                            for pi, (k0, kn, kind) in enumerate(pieces):
                                bk = self.bank()
                                self.mm(V(bk, bk[0:nq, 0:kn]), V(self.DQ, self.DQ[c * 64:(c + 1) * 64, h, c0:c0 + nq]), V(KTh, KTh[c * 64:(c + 1) * 64, k0:k0 + kn]))
                                self.red("dve", V(MXS, MXS[0:nq, c, pi:pi + 1]), V(bk, bk[0:nq, 0:kn]), ALU.max)
                            self.red("dve", V(st, st[0:nq, 40 + c:41 + c]), V(MXS, MXS[0:nq, c, 0:npc]), ALU.max)
                            self.ts("dve", V(st, st[0:nq, 42 + c:43 + c]), V(st, st[0:nq, 40 + c:41 + c]), V(self.CF, self.CF[0:nq, 4 + h:5 + h]), -1.0, ALU.add, ALU.mult)
                            self.tt("dve", V(st, st[0:nq, 44 + c:45 + c]), V(st, st[0:nq, 42 + c:43 + c]), V(self.CF, self.CF[0:nq, h:h + 1]), ALU.add)
                        for c in range(2):
                            PP = P1 if c == 0 else P2
                            self.memset("dve", V(SMS, SMS[0:nq, c, :]), 0.0)
                            for pi, (k0, kn, kind) in enumerate(pieces):
                                bk = self.bank()
                                self.mm(V(bk, bk[0:nq, 0:kn]), V(self.DQ, self.DQ[c * 64:(c + 1) * 64, h, c0:c0 + nq]), V(KTh, KTh[c * 64:(c + 1) * 64, k0:k0 + kn]))
                                if kind is None:
                                    self.act(V(PP, PP[0:nq, k0:k0 + kn]), V(bk, bk[0:nq, 0:kn]), AF.Exp, bias=V(st, st[0:nq, 44 + c:45 + c]), scale=1.0,
                                             accum=V(SMS, SMS[0:nq, c, pi:pi + 1]))
                                else:
                                    self.tt("dve", V(NB, NB[0:nq, 0:kn]), V(bk, bk[0:nq, 0:kn]), V(self.BT, self.BT[0:nq, kind, h, 0:kn]), ALU.add)
                                    self.act(V(PP, PP[0:nq, k0:k0 + kn]), V(NB, NB[0:nq, 0:kn]), AF.Exp, bias=V(st, st[0:nq, 42 + c:43 + c]), scale=1.0,
                                             accum=V(SMS, SMS[0:nq, c, pi:pi + 1]))
                            self.red("dve", V(st, st[0:nq, 46 + c:47 + c]), V(SMS, SMS[0:nq, c, 0:npc]), ALU.add)
                        self.recip(V(st, st[0:nq, 48:50]), V(st, st[0:nq, 46:48]))
                        self.tt("dve", V(st, st[0:nq, 50:51]), V(st, st[0:nq, 49:50]), V(self.LAMC, self.LAMC[0:nq, l, 1:2]), ALU.mult)
                        self.ts("dve", V(P1, P1[0:nq, 0:nk]), V(P1, P1[0:nq, 0:nk]), V(st, st[0:nq, 48:49]), None, ALU.mult)
                        self.stt("dve", V(P1, P1[0:nq, 0:nk]), V(P2, P2[0:nq, 0:nk]), V(st, st[0:nq, 50:51]), V(P1, P1[0:nq, 0:nk]), ALU.mult, ALU.add)
                        kbl = [(kb, 128) for kb in range(nk // 128)]
                        if nk % 128:
                            kbl.append((nk // 128, nk % 128))
                        for g0 in range(0, len(kbl), 4):
                            grp = kbl[g0:g0 + 4]
                            bk = self.bank()
                            for gi, (kb, nr) in enumerate(grp):
                                self.mm(V(bk, bk[0:nr, gi * 128:gi * 128 + nq]), V(P1, P1[0:nq, kb * 128:kb * 128 + nr]), V(self.identb, self.identb[0:nq, 0:nq]))
                            if all(b[1] == 128 for b in grp):
                                self.cp("act", V(AT, AT[:, g0:g0 + len(grp), 0:nq], g0, g0 + len(grp)),
                                        V(bk, bk[:, 0:len(grp) * 128].rearrange("p (a n) -> p a n", n=128)[:, :, 0:nq]))
                            else:
                                for gi, (kb, nr) in enumerate(grp):
                                    self.cp("act", V(AT, AT[0:nr, kb, 0:nq], kb, kb + 1), V(bk, bk[0:nr, gi * 128:gi * 128 + nq]))
                        bkO = self.bankO()
                        for i, (kb, nr) in enumerate(kbl):
                            self.mm(V(bkO, bkO[0:nq, 0:128]), V(AT, AT[0:nr, kb, 0:nq], kb, kb + 1), V(VH, VH[0:nr, kb, :], kb, kb + 1), i == 0, i == len(kbl) - 1)
                        self.cp("act", V(OSc, OSc[0:nq, :]), V(bkO, bkO[0:nq, 0:128]))
                        tm = self.tmp()
                        self.memset("dve", V(st, st[0:nq, 52:53]), 0.0)
                        self.act(V(tm, tm[0:nq, 0:128]), V(OSc, OSc[0:nq, :]), AF.Square, accum=V(st, st[0:nq, 52:53]))
                        self.act(V(st, st[0:nq, 53:54]), V(st, st[0:nq, 52:53]), AF.Sqrt, bias=LN_EPS, scale=1.0 / 128)
                        self.recip(V(st, st[0:nq, 54:55]), V(st, st[0:nq, 53:54]))
                        self.ts("dve", V(OSc, OSc[0:nq, :]), V(OSc, OSc[0:nq, :]), V(st, st[0:nq, 54:55]), 1.0 - lam_init, ALU.mult, ALU.mult)
                        self.tt("dve", V(YTOK, YTOK[0:nq, 0, 0:128]), V(OSc, OSc[0:nq, :]), V(self.SUBG, self.SUBG[0:nq, l, :]), ALU.mult)
                        bk = self.bank()
                        self.mm(V(bk, bk[:, 0:nq]), V(YTOK, YTOK[0:nq, 0, 0:128]), V(self.identb, self.identb[0:nq, 0:nq]))
                        self.cp("act", V(self.YT[2], self.YT[2][:, h, c0:c0 + nq]), V(bk, bk[:, 0:nq]))

    def gates(self, l, nxt=None):
        P = self.P
        Tt = self.tile_T
        X, X16 = self.X, self.X16
        with P.scope():
            ACC = P.sb("ACC", [128, 4, self.cfg.T], F32)
            M16 = P.sb("M16", [128, DC, self.cfg.T], BF16)
            SQ = P.sb("SQg", [128, DC, self.cfg.T], BF16)
            for g in range(4):
                for b in range(4):
                    sg_ = self.wload(self.w_gt, l * 4 + b, 0, 16, g * 512)
                    sb_ = self.wload(self.w_br, l * 4 + b, 0, 4, g * 512)
                    for c in range(4):
                        cg = g * 4 + c
                        bk = self.bank()
                        bkp = self.bank()
                        for k in range(DC):
                            self.mm(V(bk, bk[:, 0:Tt]), self.W(sg_, k, c * 128, (c + 1) * 128), V(X16, X16[:, k, 0:Tt], k, k + 1), k == 0, k == DC - 1)
                        for k in range(4):
                            self.mm(V(bkp, bkp[:, 0:Tt]), self.W(sb_, k, c * 128, (c + 1) * 128), V(self.YT[b], self.YT[b][:, k, 0:Tt]), k == 0, k == 3)
                        tm = self.tmp()
                        sgv = V(tm, tm[:, 0:Tt])
                        bi = (l * 4 + b) * 16 + cg
                        self.act(sgv, V(bk, bk[:, 0:Tt]), AF.Sigmoid, bias=V(self.bg, self.bg[:, bi:bi + 1]), scale=1.0)
                        acc = V(ACC, ACC[:, c, 0:Tt], c, c + 1)
                        if b == 0:
                            self.tt("dve", acc, sgv, V(bkp, bkp[:, 0:Tt]), ALU.mult)
                        else:
                            self.tt("dve", sgv, sgv, V(bkp, bkp[:, 0:Tt]), ALU.mult)
                            if b < 3:
                                self.tt("dve", acc, acc, sgv, ALU.add)
                            else:
                                self.tt("dve", V(M16, M16[:, cg, 0:Tt], cg, cg + 1), acc, sgv, ALU.add)
            for g in range(4):
                s = self.wload(self.w_o, l, 0, 16, g * 512)
                for c in range(4):
                    bk = self.bank()
                    for k in range(DC):
                        self.mm(V(bk, bk[:, 0:Tt]), self.W(s, k, c * 128, (c + 1) * 128), V(M16, M16[:, k, 0:Tt], k, k + 1), k == 0, k == DC - 1)
                    cg = g * 4 + c
                    xc = V(X, X[:, cg, 0:Tt], cg, cg + 1)
                    self.stt("dve", xc, xc, self.cfg.ALPHA, V(bk, bk[:, 0:Tt]), ALU.mult, ALU.add)
            self.prefetch(nxt)
            self.layernorm(l, 1, SQ)


def _t5_bucket(rel):
    import jax
    import jax.numpy as jnp
    cpu = jax.devices("cpu")[0]
    with jax.default_device(cpu):
        rel = jnp.asarray(rel, dtype=jnp.int32)
        nb = 16
        max_exact = 8
        n = jnp.abs(rel)
        nf = jnp.maximum(n, 1).astype(jnp.float32)
        large = max_exact + (jnp.log(nf / max_exact) / math.log(128 / max_exact) * (nb - max_exact)).astype(jnp.int32)
        large = jnp.minimum(large, nb - 1)
        out = jnp.where(rel > 0, nb, 0) + jnp.where(n < max_exact, n, large)
        return np.asarray(out)


def make_consts(cfg):
    SEQ, PAST, NS, L, T, NT = cfg.SEQ, cfg.PAST, cfg.NS, cfg.DEPTH, cfg.T, cfg.NT
    f32 = np.float32
    c = {}
    inv = (f32(10000.0) ** (-(np.arange(64, dtype=f32) / f32(64)))).astype(f32)
    rope = np.zeros((NT + 1, 2, 128, T), f32)
    dimi = np.arange(128) % 64
    sign = np.where(np.arange(128) < 64, -1.0, 1.0)

    def fill(ti, cols, pos):
        ang = (pos.astype(f32)[None, :] * inv[dimi][:, None]).astype(f32).astype(np.float64)
        rope[ti, 0][:, cols] = np.cos(ang)
        rope[ti, 1][:, cols] = np.sin(ang) * sign[:, None]
    for t in range(NT):
        fill(t, np.arange(T), np.arange(t * T, (t + 1) * T))
    for s in range(NS):
        fill(NT, 32 * s + np.arange(16), PAST + np.arange(16))
    c["rope"] = rope
    gam = np.array([1.0 - 2.0 ** (-5 - h) for h in range(4)], np.float64)
    sc = 128.0 ** -0.5
    qd = np.zeros((2, 128, 4, T), np.float64)
    for h in range(4):
        qd[0, :, h, :] = gam[h] ** ((np.arange(T) % 64) + 1)[None, :]
        for s in range(NS):
            qd[1, :, h, 32 * s:32 * s + 16] = gam[h] ** (np.arange(16) + 1)[None, :]
    c["qd"] = qd.reshape(2, 128, 4 * T).astype(f32).astype(BF)
    rm = np.zeros((2, 128, 4, 128), np.float64)
    kdx = np.zeros((2, 128, 4, 128), np.float64)
    j = np.arange(128)[:, None]
    i = np.arange(128)[None, :]
    for h in range(4):
        ok = (j // 64 == i // 64) & ((j % 64) <= (i % 64))
        rm[0, :, h, :] = np.where(ok, sc * gam[h] ** (-((j % 64) + 1.0)), 0.0)
        kdx[0, :, h, :] = (sc * gam[h] ** (63.0 - (np.arange(128) % 64)))[:, None]
        for s in range(NS):
            jl = np.arange(16)[:, None]
            il = np.arange(16)[None, :]
            rm[1, 32 * s:32 * s + 16, h, 32 * s:32 * s + 16] = np.where(jl <= il, sc * gam[h] ** (-(jl + 1.0)), 0.0)
            kdx[1, 32 * s:32 * s + 16, h, :] = (sc * gam[h] ** (15.0 - np.arange(16)))[:, None]
    c["rm"] = rm.reshape(2, 128, 512).astype(f32)
    c["kdx"] = kdx.reshape(2, 128, 512).astype(f32)
    cd = np.zeros((2, 128, 4), np.float64)
    cd[0] = (gam ** 64)[None, :]
    cd[1] = (gam ** 16)[None, :]
    c["cdec"] = cd.astype(f32)
    q = np.arange(128)[None, :]
    k = np.arange(128)[:, None]
    oh = np.zeros((32, 2, 128, 128), f32)
    for ti in range(2):
        bk = _t5_bucket(k - q - 128 * ti)
        for b in range(32):
            oh[b, ti] = (bk == b)
    c["oh"] = oh.reshape(32, 2 * 128 * 128)
    qq = np.arange(128)[:, None]
    kk = np.arange(128)[None, :]
    c["mask0"] = np.where((kk // 64) > (qq // 64), NEG, 0.0).astype(f32)
    c["ident"] = np.eye(128, dtype=f32)
    c["identb"] = np.eye(128, dtype=f32).astype(BF)
    c["ones"] = np.full((128, 128), 1.0 / D, f32).astype(BF)
    return c


def prep_shared(inp, cfg):
    L = cfg.DEPTH
    f = lambda a: np.ascontiguousarray(np.asarray(a, dtype=np.float32))
    sh = {}
    def relay_up(w):
        w = f(w)
        a = w[:, :, :DFF].reshape(w.shape[0], D, 22, 256)
        b = w[:, :, DFF:].reshape(w.shape[0], D, 22, 256)
        return np.ascontiguousarray(np.concatenate([a, b], axis=3).reshape(w.shape[0], D, 2 * DFF))
    sh["w_up1"] = relay_up(inp["ffn1_w_up"]); sh["w_dn1"] = f(inp["ffn1_w_down"])
    sh["w_up2"] = relay_up(inp["ffn2_w_up"]); sh["w_dn2"] = f(inp["ffn2_w_down"])
    w_in = f(inp["w_in"])
    sl = lambda i: w_in[:, :, i * 512:(i + 1) * 512]
    swp = np.concatenate([np.arange(64, 128), np.arange(0, 64)])
    perm = np.concatenate([h * 128 + swp for h in range(4)])
    sh["w_in"] = np.ascontiguousarray(np.concatenate(
        [sl(0), sl(1), sl(2), sl(3), sl(3)[:, :, perm], sl(4), sl(4)[:, :, perm], sl(5), sl(6), sl(7), sl(8), sl(9), sl(10)], axis=2))
    sh["w_mkv"] = f(inp["w_mem_kv"])
    sh["w_br"] = f(inp["w_branch"]).reshape(L * 4, 512, D)
    sh["w_gt"] = f(inp["w_gate"]).reshape(L * 4, D, D)
    sh["w_o"] = f(inp["w_o"])

    def pc(a):
        a = f(a)
        lead = a.shape[:-1]
        a = a.reshape(lead + (16, 128))
        a = np.moveaxis(a, -1, 0)
        return np.ascontiguousarray(a).reshape(128, -1)
    lnp = np.stack([f(inp[k]) for k in ("ln1_g", "ln1_b", "ln2_g", "ln2_b", "ln3_g", "ln3_b")], axis=1)
    sh["lnp"] = pc(lnp)
    sh["bgate"] = pc(inp["b_gate"])
    cw = f(inp["conv_w"]).reshape(L, 3, 4, 128)
    sh["convw"] = np.ascontiguousarray(np.moveaxis(cw, -1, 0)).reshape(128, -1)
    sh["gn"] = f(inp["ret_gn_g"]); sh["subg"] = f(inp["diff_subln_g"])
    sh["lam"] = f(inp["diff_lambda"]).reshape(L, 256)
    sh["relb"] = f(inp["rel_bias"]).reshape(1, 128)
    sh["relb2"] = f(inp["rel_bias"])
    sh.update(make_consts(cfg))
    return sh


def prep_core(inp, cfg, core, nb):
    f = lambda a: np.ascontiguousarray(np.asarray(a, dtype=np.float32))
    NS, L = cfg.NS, cfg.DEPTH
    b = core % nb
    s0 = core * NS
    m = {}
    m["xp"] = f(inp["x_prompt"][b])
    m["xs"] = f(inp["x_sample"][s0:s0 + NS]).reshape(NS * 16, D)
    m["memp"] = f(inp["mem_prompt"][b])
    m["st_conv"] = f(inp["state_conv"][:, s0:s0 + NS])
    m["st_ret"] = f(inp["state_ret"][:, s0:s0 + NS])
    m["c_k"] = f(inp["cache_diff_k"][:, s0:s0 + NS]).reshape(L, NS, cfg.PAST, 512)
    m["c_v"] = f(inp["cache_diff_v"][:, s0:s0 + NS]).reshape(L, NS, cfg.PAST, 512)
    m["c_mk"] = f(inp["cache_mem_k"][:, s0:s0 + NS]).reshape(L, NS, 256, 512)
    m["c_mv"] = f(inp["cache_mem_v"][:, s0:s0 + NS]).reshape(L, NS, 256, 512)
    return m


def assemble(results, cfg, nb, ncores):
    L, NS, SEQ = cfg.DEPTH, cfg.NS, cfg.SEQ
    R = results
    pc = list(range(nb))
    f = lambda a: np.asarray(a, dtype=np.float32)
    yp = np.stack([f(R[c]["yp"]) for c in pc])
    ys = np.concatenate([f(R[c]["ys"]).reshape(NS, 16, D) for c in range(ncores)])
    conv_p = np.stack([f(R[c]["o_conv_p"]) for c in pc], axis=1)
    ret_p = np.stack([f(R[c]["o_ret_p"]) for c in pc], axis=1)
    dk_p = np.stack([f(R[c]["o_dk_p"]).reshape(L, SEQ, 4, 128) for c in pc], axis=1)
    dv_p = np.stack([f(R[c]["o_dv_p"]).reshape(L, SEQ, 4, 128) for c in pc], axis=1)
    mk_p = np.stack([f(R[c]["o_mk_p"]).reshape(L, 256, 4, 128) for c in pc], axis=1)
    mv_p = np.stack([f(R[c]["o_mv_p"]).reshape(L, 256, 4, 128) for c in pc], axis=1)
    conv_s = np.concatenate([f(R[c]["o_conv_s"]) for c in range(ncores)], axis=1)
    ret_s = np.concatenate([f(R[c]["o_ret_s"]) for c in range(ncores)], axis=1)
    dk_s = np.concatenate([f(R[c]["o_dk_s"]).reshape(L, NS, 16, 4, 128) for c in range(ncores)], axis=1)
    dv_s = np.concatenate([f(R[c]["o_dv_s"]).reshape(L, NS, 16, 4, 128) for c in range(ncores)], axis=1)
    return (yp, ys, conv_p, ret_p, dk_p, dv_p, mk_p, mv_p, conv_s, ret_s, dk_s, dv_s)


def kernel(**inputs):
    xp = inputs["x_prompt"]
    nb, SEQ = xp.shape[0], xp.shape[1]
    ncores = 8
    NS = inputs["x_sample"].shape[0] // ncores
    cfg = Cfg(SEQ=SEQ, PAST=inputs["cache_diff_k"].shape[2], NS=NS, DEPTH=inputs["w_o"].shape[0], T=512)
    nc = KB(cfg).build()
    sh = prep_shared(inputs, cfg)
    in_maps = []
    for c in range(ncores):
        m = dict(sh)
        m.update(prep_core(inputs, cfg, c, nb))
        in_maps.append(m)
    res = run_bass_kernel_spmd(nc, in_maps, core_ids=list(range(ncores)))
    return assemble(res.results, cfg, nb, ncores)
```
